# Optimizing a Trainium2 kernel written in Bass

```python
import jax, jax.numpy as jnp
from jax import lax
import numpy as np

D_MODEL = 1024
BATCH = 8
SEQ = 2048
DEPTH = 1
DEC_BATCH = 8
DEC_SEQ = 8192
PAST_LEN = 128

HEAD_DIM = 64
A_HEADS = 8
A_KV_HEADS = 2
A_WINDOW = 128
B_GROUPS = ((128, 1), (512, 4), (2048, 16))
B_HEADS_PER_GROUP = 4
B_HEADS = B_HEADS_PER_GROUP * len(B_GROUPS)
N_BIAS_HEADS = A_HEADS + B_HEADS
REL_BUCKETS = 32
REL_MAX_DISTANCE = 1024
D_FF = 4 * D_MODEL
A_Q = A_HEADS * HEAD_DIM
A_KV = A_KV_HEADS * HEAD_DIM
B_W = B_HEADS * HEAD_DIM
B_OUT = B_HEADS_PER_GROUP * HEAD_DIM
IN_COLS = A_Q + 2 * A_KV + 3 * B_W + 2 * D_MODEL
RMS_EPS = 1e-6
NEG_INF = -1e30

kernel_name = 'hybrid_gated_window_dilated_encoder'


def _rmsnorm(x, g):
    x32 = x.astype(jnp.float32)
    y = x32 * lax.rsqrt(jnp.mean(x32 * x32, axis=-1, keepdims=True) + RMS_EPS)
    return (y * g.astype(jnp.float32)).astype(x.dtype)


def _rel_bucket(rel):
    half = REL_BUCKETS // 2
    max_exact = half // 2
    n = np.abs(rel)
    large = max_exact + (np.log(np.maximum(n, 1) / max_exact)
                         / np.log(REL_MAX_DISTANCE / max_exact) * (half - max_exact)).astype(np.int32)
    large = np.minimum(large, half - 1)
    return (np.where(rel > 0, half, 0) + np.where(n < max_exact, n, large)).astype(np.int32)


def _banded_attention(q, k, v, n, dist_scale, bias_table, sink):
    b, L, H, dh = q.shape
    G = k.shape[2]
    r = H // G
    nb = -(-L // n)
    Lp = nb * n
    qb = jnp.pad(q, ((0, 0), (0, Lp - L), (0, 0), (0, 0))).reshape(b, nb, n, G, r, dh)
    kv_pad = ((0, 0), (n, Lp - L + n), (0, 0), (0, 0))
    kp = jnp.pad(k, kv_pad).reshape(b, nb + 2, n, G, dh)
    vp = jnp.pad(v, kv_pad).reshape(b, nb + 2, n, G, dh)
    kb = jnp.concatenate([kp[:, :-2], kp[:, 1:-1], kp[:, 2:]], axis=2)
    vb = jnp.concatenate([vp[:, :-2], vp[:, 1:-1], vp[:, 2:]], axis=2)
    s = jnp.einsum('bnqgrd,bnkgd->bngrqk', qb, kb).astype(jnp.float32) * (dh ** -0.5)
    rel = np.arange(3 * n)[None, :] - n - np.arange(n)[:, None]
    bias = bias_table.astype(jnp.float32)[_rel_bucket(rel * dist_scale)]
    bias = jnp.transpose(bias.reshape(n, 3 * n, G, r), (2, 3, 0, 1))
    kpos = np.arange(nb)[:, None] * n + np.arange(3 * n)[None, :] - n
    valid = (np.abs(rel) <= n)[None] & ((kpos >= 0) & (kpos < L))[:, None, :]
    logits = jnp.where(valid[None, :, None, None], s + bias, NEG_INF)
    m = jnp.max(logits, axis=-1)
    if sink is not None:
        sink_r = sink.astype(jnp.float32).reshape(G, r)[..., None]
        m = jnp.maximum(m, sink_r)
    p = jnp.exp(logits - m[..., None])
    denom = jnp.sum(p, axis=-1)
    if sink is not None:
        denom = denom + jnp.exp(sink_r - m)
    o = jnp.einsum('bngrqk,bnkgd->bnqgrd', p.astype(vb.dtype), vb).astype(jnp.float32)
    o = o / jnp.transpose(denom, (0, 1, 4, 2, 3))[..., None]
    o = o.reshape(b, Lp, H, dh)[:, :L].astype(q.dtype)
    lse = jnp.transpose(m + jnp.log(denom), (0, 1, 4, 2, 3)).reshape(b, Lp, H)[:, :L]
    return o, lse


def _to_residues(t, dil):
    b, S = t.shape[:2]
    t = jnp.moveaxis(t.reshape((b, S // dil, dil) + t.shape[2:]), 2, 1)
    return t.reshape((b * dil, S // dil) + t.shape[3:])


def _from_residues(t, dil):
    bd, L = t.shape[:2]
    t = jnp.moveaxis(t.reshape((bd // dil, dil, L) + t.shape[2:]), 1, 2)
    return t.reshape((bd // dil, L * dil) + t.shape[3:])


def _dilated_mixture(q, k, v, bias_b):
    b, S, _, dh = q.shape
    outs, lses = [], []
    for gi, (window, dil) in enumerate(B_GROUPS):
        hs = slice(gi * B_HEADS_PER_GROUP, (gi + 1) * B_HEADS_PER_GROUP)
        o, lse = _banded_attention(_to_residues(q[:, :, hs], dil), _to_residues(k[:, :, hs], dil),
                                   _to_residues(v[:, :, hs], dil), window // (2 * dil), dil,
                                   bias_b[:, hs], None)
        outs.append(_from_residues(o, dil))
        lses.append(_from_residues(lse, dil))
    alpha = jax.nn.softmax(jnp.stack(lses), axis=0)
    o = jnp.sum(alpha[..., None] * jnp.stack(outs).astype(jnp.float32), axis=0)
    return o.astype(q.dtype).reshape(b, S, B_OUT)


def _layer(x, rel_bias, g_mix, w_in, b_gate, w_branch_a, w_branch_b, w_out, sink, g_mlp, w_up, w_down):
    b, S, _ = x.shape
    h = _rmsnorm(x, g_mix)
    z = h @ w_in
    o1 = A_Q
    o2 = o1 + A_KV
    o3 = o2 + A_KV
    o4 = o3 + B_W
    o5 = o4 + B_W
    o6 = o5 + B_W
    qa = z[..., :o1].reshape(b, S, A_HEADS, HEAD_DIM)
    ka = z[..., o1:o2].reshape(b, S, A_KV_HEADS, HEAD_DIM)
    va = z[..., o2:o3].reshape(b, S, A_KV_HEADS, HEAD_DIM)
    qb = z[..., o3:o4].reshape(b, S, B_HEADS, HEAD_DIM)
    kb = z[..., o4:o5].reshape(b, S, B_HEADS, HEAD_DIM)
    vb = z[..., o5:o6].reshape(b, S, B_HEADS, HEAD_DIM)
    gates = jax.nn.sigmoid((z[..., o6:] + b_gate).astype(jnp.float32)).astype(x.dtype)
    gates = gates.reshape(b, S, 2, D_MODEL)
    o_a, _ = _banded_attention(qa, ka, va, A_WINDOW, 1, rel_bias[:, :A_HEADS], sink)
    o_b = _dilated_mixture(qb, kb, vb, rel_bias[:, A_HEADS:])
    merged = gates[:, :, 0] * (o_a.reshape(b, S, A_Q) @ w_branch_a) + gates[:, :, 1] * (o_b @ w_branch_b)
    x = x + merged @ w_out
    h = _rmsnorm(x, g_mlp)
    return x + jnp.square(jax.nn.relu(h @ w_up)) @ w_down


def _encoder(x, rel_bias, g_mix, w_in, b_gate, w_branch_a, w_branch_b, w_out, attn_sink, g_mlp, w_up, w_down, g_final):
    for l in range(DEPTH):
        x = _layer(x, rel_bias, g_mix[l], w_in[l], b_gate[l], w_branch_a[l], w_branch_b[l], w_out[l],
                   attn_sink[l], g_mlp[l], w_up[l], w_down[l])
    return _rmsnorm(x, g_final)


def setup_inputs(seed: int = 0) -> dict:
    key = jax.random.key(seed)
    ks = jax.random.split(key, 14)

    def nrm(k, shape, scale):
        return scale * jax.random.normal(k, shape, jnp.float32)

    return {
        'x_prompt': nrm(ks[0], (BATCH, SEQ, D_MODEL), 1.0),
        'x_sample': nrm(ks[1], (DEC_BATCH, DEC_SEQ, D_MODEL), 1.0),
        'rel_bias': nrm(ks[2], (REL_BUCKETS, N_BIAS_HEADS), 0.5),
        'g_mix': 1.0 + nrm(ks[3], (DEPTH, D_MODEL), 0.05),
        'w_in': nrm(ks[4], (DEPTH, D_MODEL, IN_COLS), D_MODEL ** -0.5),
        'b_gate': nrm(ks[5], (DEPTH, 2 * D_MODEL), 0.1),
        'w_branch_a': nrm(ks[6], (DEPTH, A_Q, D_MODEL), A_Q ** -0.5),
        'w_branch_b': nrm(ks[7], (DEPTH, B_OUT, D_MODEL), B_OUT ** -0.5),
        'w_out': nrm(ks[8], (DEPTH, D_MODEL, D_MODEL), D_MODEL ** -0.5),
        'attn_sink': nrm(ks[9], (DEPTH, A_HEADS), 0.5),
        'g_mlp': 1.0 + nrm(ks[10], (DEPTH, D_MODEL), 0.05),
        'w_up': nrm(ks[11], (DEPTH, D_MODEL, D_FF), D_MODEL ** -0.5),
        'w_down': nrm(ks[12], (DEPTH, D_FF, D_MODEL), D_FF ** -0.5),
        'g_final': 1.0 + nrm(ks[13], (D_MODEL,), 0.05),
    }


def reference(x_prompt, x_sample, rel_bias, g_mix, w_in, b_gate, w_branch_a, w_branch_b, w_out,
              attn_sink, g_mlp, w_up, w_down, g_final):
    y_prompt = _encoder(x_prompt, rel_bias, g_mix, w_in, b_gate, w_branch_a, w_branch_b, w_out,
                        attn_sink, g_mlp, w_up, w_down, g_final)
    y_sample = _encoder(x_sample, rel_bias, g_mix, w_in, b_gate, w_branch_a, w_branch_b, w_out,
                        attn_sink, g_mlp, w_up, w_down, g_final)
    return (y_prompt, y_sample)
```

```python
import numpy as np
from contextlib import ExitStack
import concourse.bass as bass
import concourse.mybir as mybir
from concourse.bass_utils import run_bass_kernel_spmd

F32 = mybir.dt.float32
BF16 = mybir.dt.bfloat16
AF = mybir.ActivationFunctionType
ALU = mybir.AluOpType

D = 1024
T = 512
NCORES = 8
SEQ_P = 2048
SEQ_S = 8192
NSLAB = 33
S_KV0, S_Q0, S_M0, S_O0, S_U0, S_D0 = 0, 4, 7, 15, 17, 25
NW = 3
EPS = 1e-6
MASKV = -30000.0


def _rel_bucket(rel):
    half, max_exact = 16, 8
    n = np.abs(rel)
    large = max_exact + (np.log(np.maximum(n, 1) / max_exact) / np.log(1024 / max_exact) * (half - max_exact)).astype(np.int32)
    large = np.minimum(large, half - 1)
    return (np.where(rel > 0, half, 0) + np.where(n < max_exact, n, large)).astype(np.int32)


def _bias_tables(rel_bias):
    rb = np.concatenate([rel_bias.astype(np.float32), np.full((1, rel_bias.shape[1]), MASKV, np.float32)], axis=0)
    k = np.arange(128)[:, None, None]
    j = np.arange(3)[None, :, None]
    q = np.arange(128)[None, None, :]
    rel = (j - 1) * 128 + k - q

    def tab(nwin, scale, col0, nh):
        idx = np.where(np.abs(rel) <= nwin, _rel_bucket(rel * scale), 32)
        out = np.stack([rb[idx, col0 + h] for h in range(nh)], axis=1)
        return np.ascontiguousarray(out.reshape(128, -1))

    tabA = tab(128, 1, 0, 8)
    tabg0 = tab(64, 1, 8, 4)
    tabg1 = tab(64, 4, 12, 4)
    rl = np.arange(128) % 4
    jj = np.arange(128) // 4
    idx2 = np.full((128, 5, 128), 32, np.int64)
    for di, dl in enumerate((-2, -1, 0, 1, 2)):
        r2 = 32 * dl + jj[:, None] - jj[None, :]
        ok = (rl[:, None] == rl[None, :]) & (np.abs(r2) <= 64)
        idx2[:, di, :] = np.where(ok, _rel_bucket(r2 * 16), 32)
    tab2 = np.stack([rb[idx2, 16 + h] for h in range(4)], axis=2)
    tab2 = np.ascontiguousarray(tab2.reshape(128, -1))
    return tabA, np.concatenate([tabg0, tabg1], axis=1), tab2


def _pkc(w):
    K = w.shape[0] // 128
    return np.ascontiguousarray(w.reshape(K, 128, -1).transpose(1, 0, 2)).reshape(128, -1)


def _slabs(w_in, w_a, w_b, w_out, w_up, w_down):
    sl = np.zeros((NSLAB, 128, 4096), np.float32)

    def put_fm(s, cols):
        sub = np.zeros((1024, 512), np.float32)
        sub[:, :len(cols)] = w_in[:, cols]
        sl[s] = _pkc(sub)

    ar = np.arange
    kcols = np.concatenate([ar(512, 640), ar(1536, 2304)])
    put_fm(S_KV0 + 0, kcols[:512])
    put_fm(S_KV0 + 1, kcols[512:])
    put_fm(S_KV0 + 2, np.concatenate([ar(640, 768), ar(2304, 2560)]))
    put_fm(S_KV0 + 3, ar(2560, 3072))
    qa = np.concatenate([np.concatenate([ar(i * 64, i * 64 + 64), ar((i + 4) * 64, (i + 4) * 64 + 64)]) for i in range(4)])
    put_fm(S_Q0 + 0, qa)
    put_fm(S_Q0 + 1, ar(768, 1280))
    put_fm(S_Q0 + 2, ar(1280, 1536))
    for j in range(8):
        g = np.zeros((1024, 256), np.float32)
        g[:, :128] = w_in[:, 3072 + j * 128:3072 + j * 128 + 128]
        g[:, 128:] = w_in[:, 4096 + j * 128:4096 + j * 128 + 128]
        sl[S_M0 + j, :, 0:2048] = _pkc(g)
        sl[S_M0 + j, :, 2048:2560] = _pkc(w_a[:, j * 128:j * 128 + 128])
        sl[S_M0 + j, :, 2560:2816] = _pkc(w_b[:, j * 128:j * 128 + 128])
    for h in range(2):
        sl[S_O0 + h] = _pkc(w_out[:, h * 512:h * 512 + 512])
    for i in range(8):
        sl[S_U0 + i] = _pkc(w_up[:, i * 512:i * 512 + 512])
    for h in range(2):
        for kg in range(4):
            sl[S_D0 + h * 4 + kg] = _pkc(w_down[kg * 1024:(kg + 1) * 1024, h * 512:h * 512 + 512])
    return sl.reshape(NSLAB * 128, 4096)


ENGS = ("pe", "act", "dve", "pool", "sp")


class Buf:
    __slots__ = ("w", "r")

    def __init__(self):
        self.w = []
        self.r = []


class Planner:
    def __init__(self, nc, es):
        self.nc = nc
        self.es = es
        self.ops = {e: [] for e in ENGS}
        self.sem = {}
        self.cnt = {}
        self.waited = {e: {} for e in ENGS}
        self.label = ""
        self.tokmap = {}
        self.semorder = []
        for e in ("pe", "act", "dve", "pool"):
            self.newsem(e)

    def newsem(self, name):
        self.sem[name] = self.es.enter_context(self.nc.semaphore("s_" + name))
        self.cnt[name] = 0
        self.semorder.append(name)

    def _waits(self, eng, toks):
        best = {}
        for tk in toks:
            if tk is None:
                continue
            s, v = tk
            if eng == "pe" and s == "pe":
                continue
            if self.waited[eng].get(s, 0) >= v:
                continue
            if best.get(s, 0) < v:
                best[s] = v
        for s, v in best.items():
            self.waited[eng][s] = v
        return list(best.items())

    def emit(self, eng, fn, waits=(), sig=True, force=None):
        wl = self._waits(eng, waits)
        if force is not None:
            wl = wl + [force]
        tok = None
        if sig:
            self.cnt[eng] += 1
            tok = (eng, self.cnt[eng])
            self.tokmap[tok] = self.label
        sems = self.sem

        def run(e):
            for s, v in wl:
                e.wait_ge(sems[s], v)
            if fn is not None:
                ins = fn(e)
                if sig:
                    ins.then_inc(sems[eng], 1)

        self.ops[eng].append(run)
        return tok

    def dma(self, q, out, in_, semname, waits=()):
        wl = self._waits(q, waits)
        self.cnt[semname] += 16
        tok = (semname, self.cnt[semname])
        self.tokmap[tok] = self.label
        sems = self.sem

        def run(e):
            for s, v in wl:
                e.wait_ge(sems[s], v)
            e.dma_start(out=out, in_=in_).then_inc(sems[semname], 16)

        self.ops[q].append(run)
        return tok


def deps(reads=(), writes=()):
    t = []
    for b in reads:
        t += b.w
    for b in writes:
        t += b.w
        t += b.r
    return t


def commit(tok, reads=(), writes=()):
    for b in reads:
        b.r.append(tok)
        if len(b.r) > 8:
            best = {}
            for s, v in b.r:
                if best.get(s, 0) < v:
                    best[s] = v
            b.r = list(best.items())
    for b in writes:
        b.w = [tok]
        b.r = []


def build_program(seqs, dbg_c=None, stop=None):
    nc = bass.Bass("TRN2", target_bir_lowering=False)
    dr = {}
    dbgt = {}
    if dbg_c is not None:
        for nm, w_ in (("d_QT", 5120), ("d_oTA", 2048), ("d_oTB", 1024), ("d_mT", 4096), ("d_hT", 4096), ("d_aT", 16384)):
            dbgt[nm] = nc.dram_tensor(nm, [128, w_], BF16, kind="ExternalOutput").ap()
        dbgt["d_x1"] = nc.dram_tensor("d_x1", [128, 4096], F32, kind="ExternalOutput").ap()
    for name, yname, S in seqs:
        dr[name] = nc.dram_tensor(name, [S, D], F32, kind="ExternalInput").ap()
        dr[yname] = nc.dram_tensor(yname, [S, D], F32, kind="ExternalOutput").ap()
    wsl = nc.dram_tensor("wsl", [NSLAB * 128, 4096], F32, kind="ExternalInput").ap()
    tabA_d = nc.dram_tensor("tabA", [128, 8 * 3 * 128], F32, kind="ExternalInput").ap()
    tabB_d = nc.dram_tensor("tabB", [128, 2 * 4 * 3 * 128], F32, kind="ExternalInput").ap()
    tab2_d = nc.dram_tensor("tab2", [128, 5 * 4 * 128], F32, kind="ExternalInput").ap()
    cvec_d = nc.dram_tensor("cvec", [128, 64], F32, kind="ExternalInput").ap()
    gfin_d = nc.dram_tensor("gfin", [128, D], F32, kind="ExternalInput").ap()
    wscr = nc.dram_tensor("wscr", [NSLAB * 128, 4096], BF16, kind="Internal").ap()

    es = ExitStack()
    with es:
        off = [0]

        def A(nbytes):
            o = off[0]
            off[0] += (nbytes + 31) // 32 * 32
            return o

        o_ident = A(256)
        o_ones = A(128)
        o_zero = A(1024)
        o_ebA = A(8 * 3 * 128 * 4)
        o_ebB = A(2 * 4 * 3 * 128 * 4)
        o_eb2 = A(5 * 4 * 128 * 4)
        o_gfin = A(4096)
        o_cvec = A(256)
        o_esink = A(32)
        o_stat = A(1024)
        o_identf = A(512)
        o_ktr = A(4 * 5 * 512 * 2)
        o_kt2 = A(5 * 2 * 512 * 2)
        o_vr = A(4 * 4 * 640 * 2)
        o_v2 = A(5 * 4 * 256 * 2)
        o_xtmp = A(2 * 4096)
        o_xres = A(16384)
        o_hb = A(2 * 2048)
        o_hT = A(2 * 8192)
        o_R = A(36864)
        o_w = A(NW * 8192)
        total = off[0]
        AW = total // 4
        arena = es.enter_context(nc.sbuf_tensor("arena", [128, AW], F32))
        arena_b = arena.bitcast(BF16)

        def VF(o, n):
            return arena[:, o // 4:o // 4 + n]

        def VB(o, n):
            return arena_b[:, o // 2:o // 2 + n]

        ident = VB(o_ident, 128)
        ones = VB(o_ones, 64)
        zeros = VB(o_zero, 512)
        identf = VF(o_identf, 128)
        ebA = VF(o_ebA, 8 * 3 * 128).rearrange("p (h x) -> p h x", h=8)
        ebB = VF(o_ebB, 2 * 4 * 3 * 128).rearrange("p (g h x) -> p g h x", g=2, h=4)
        gfin = VF(o_gfin, 1024)
        cvec = VF(o_cvec, 64)
        gm = cvec[:, 0:8]
        gl = cvec[:, 8:16]
        bg = cvec[:, 16:32]
        sinkp = cvec[:, 32:36]
        esink = VF(o_esink, 4)
        stat = VF(o_stat, 256)
        KTr = VB(o_ktr, 4 * 5 * 512).rearrange("p (s b t) -> p s b t", s=4, b=5)
        KT2r = VB(o_kt2, 5 * 2 * 512).rearrange("p (s b t) -> p s b t", s=5, b=2)
        Vr = VB(o_vr, 4 * 4 * 640).rearrange("p (s t c) -> p s t c", s=4, t=4)
        V2r = VB(o_v2, 5 * 4 * 256).rearrange("p (s r c) -> p s r c", s=5, r=4)
        xtmp = [VF(o_xtmp + i * 4096, 1024) for i in range(2)]
        xres = [VF(o_xres + t * 4096, 1024) for t in range(4)]
        hb = [VB(o_hb + i * 2048, 1024) for i in range(2)]
        hTs = [VB(o_hT + i * 8192, 4096).rearrange("p (k t) -> p k t", k=8) for i in range(2)]
        QT = VB(o_R, 5120).rearrange("p (b t) -> p b t", b=10)
        mergedT = VB(o_R, 4096).rearrange("p (b t) -> p b t", b=8)
        oTA = VB(o_R + 10240, 2048).rearrange("p (b t) -> p b t", b=4)
        oTB = VB(o_R + 18432, 1024).rearrange("p (b t) -> p b t", b=2)
        Ebuf = [VF(o_R + 22528 + i * 2048, 512) for i in range(2)]
        PTb = [VB(o_R + 22528 + i * 1024, 512) for i in range(8)]
        rdb = [VF(o_R + 30720 + i * 2048, 512) for i in range(2)]
        sgb = [VF(o_R + 22528 + i * 2048, 512) for i in range(4)]
        tpb = [VF(o_R + 30720 + i * 2048, 512) for i in range(2)]
        aT = VB(o_R, 16384).rearrange("p (b t) -> p b t", b=32)
        relb = [VF(o_R + 32768 + i * 2048, 512) for i in range(2)]
        wslot = [VB(o_w + i * 8192, 4096) for i in range(NW)]
        stgA = VF(o_xres, 4096)
        stgB = VF(o_R, 4096)

        psum = [es.enter_context(nc.psum_tensor("ps%d" % i, [128, 512], F32)) for i in range(8)]
        psum_b = [p.bitcast(BF16) for p in psum]

        P = Planner(nc, es)
        for i in range(NW):
            P.newsem("w%d" % i)
        for i in range(2):
            P.newsem("xt%d" % i)
        for t in range(4):
            P.newsem("xr%d" % t)
            P.newsem("ys%d" % t)
        for s_ in range(NSLAB):
            P.newsem("cs%d" % s_)
        for nm in ("tb", "tb2", "d_QT", "d_oTA", "d_oTB", "d_hT", "d_mT", "d_x1", "d_aT"):
            P.newsem(nm)

        B_ps = [Buf() for _ in range(8)]
        B_w = [Buf() for _ in range(NW)]
        B_xtmp = [Buf(), Buf()]
        B_xres = [Buf() for _ in range(4)]
        B_hb = [Buf(), Buf()]
        B_hTs = [[Buf() for _ in range(4)] for _ in range(2)]
        B_KT = [Buf() for _ in range(4)]
        B_KT2 = [Buf() for _ in range(5)]
        B_V = [Buf() for _ in range(4)]
        B_V2 = [Buf() for _ in range(5)]
        B_QT = [Buf() for _ in range(10)]
        B_E = [Buf(), Buf()]
        B_PT = [Buf() for _ in range(8)]
        B_rd = [Buf(), Buf()]
        B_oTA = [Buf() for _ in range(4)]
        B_oTB = [Buf(), Buf()]
        B_sg = [Buf() for _ in range(4)]
        B_tp = [Buf(), Buf()]
        B_mT = [Buf() for _ in range(8)]
        B_aT = [Buf() for _ in range(32)]
        B_rel = [Buf(), Buf()]
        B_stg = [Buf(), Buf()]
        B_const = Buf()
        B_eb = Buf()
        B_junk = Buf()

        free_banks = list(range(8))

        def acquire():
            assert free_banks, "out of PSUM banks"
            return free_banks.pop(0)

        def release(b):
            free_banks.append(b)

        rr = {"E": 0, "PT": 0, "rd": 0, "sg": 0, "tp": 0, "rel": 0, "xt": 0, "st": 0, "w": 0}

        def nxt(key, n):
            v = rr[key]
            rr[key] = (v + 1) % n
            return v

        wcontent = {}
        wserial = [0]

        def wget(slab):
            i = nxt("w", NW)
            wserial[0] += 1
            wcontent[i] = wserial[0]
            cast_upto(slab + 6)
            tok = P.dma("sp", wslot[i], wscr[slab * 128:(slab + 1) * 128, :], "w%d" % i, waits=deps(writes=[B_w[i]]) + [cast_tok[slab]])
            commit(tok, writes=[B_w[i]])
            return wslot[i], B_w[i], i, wserial[0]

        def mm_group(mms, reads, bank, accumulate_into=False):
            w = deps(reads=reads, writes=[] if accumulate_into else [B_ps[bank]])
            if accumulate_into:
                w = w + deps(reads=[B_ps[bank]])
            tok = None
            for i, f in enumerate(mms):
                tok = P.emit("pe", f, waits=w if i == 0 else (), sig=(i == len(mms) - 1))
            commit(tok, reads=reads)
            B_ps[bank].w = [tok]
            if not accumulate_into:
                B_ps[bank].r = []
            return tok

        def MM(out, lhsT, rhs, start, stop, tp=None):
            if tp is None:
                return lambda e: e.matmul(out, lhsT=lhsT, rhs=rhs, start=start, stop=stop)
            return lambda e: e.matmul(out, lhsT=lhsT, rhs=rhs, start=start, stop=stop, tile_position=tp,
                                      skip_group_check=True)

        def op(eng, fn, reads=(), writes=()):
            tok = P.emit(eng, fn, waits=deps(reads=reads, writes=writes))
            commit(tok, reads=reads, writes=writes)
            return tok

        ptoks = []
        t_c1 = P.dma("pool", cvec, cvec_d, "tb")
        t_tb = P.dma("pool", gfin, gfin_d, "tb")
        t = P.dma("pool", VF(o_ebA, 8 * 3 * 128), tabA_d, "tb2")
        t = P.dma("pool", VF(o_ebB, 2 * 4 * 3 * 128), tabB_d, "tb2")
        t_tb2 = P.dma("pool", VF(o_eb2, 5 * 4 * 128), tab2_d, "tb2")
        t0 = P.emit("pool", lambda e: e.memset(identf, 0.0))
        t1 = P.emit("pool", lambda e: e.affine_select(out=identf, in_=identf, pattern=[[-1, 128]], compare_op=ALU.not_equal,
                                                      fill=1.0, base=0, channel_multiplier=1), waits=[t0])
        ptoks.append(P.emit("act", lambda e: e.activation(out=ident, in_=identf, func=AF.Copy), waits=[t1]))
        ptoks.append(P.emit("dve", lambda e: e.memset(ones, 1.0)))
        ptoks.append(P.emit("dve", lambda e: e.memset(zeros, 0.0)))
        ptoks.append(P.emit("act", lambda e: e.activation(out=esink, in_=sinkp, func=AF.Exp), waits=[t_tb]))
        ptoks.append(t_tb)
        for (o_, n_) in ((o_ebA, 3072), (o_ebB, 3072), (o_eb2, 2560)):
            v = VF(o_, n_)
            B_eb.w.append(P.emit("act", (lambda v: lambda e: e.activation(out=v, in_=v, func=AF.Exp))(v), waits=[t_tb2]))

        cast_tok = {}
        cast_next = [0]

        def cast_upto(slab):
            while cast_next[0] <= min(slab, NSLAB - 1):
                s_ = cast_next[0]
                cast_tok[s_] = P.dma("pool", wscr[s_ * 128:(s_ + 1) * 128, :], wsl[s_ * 128:(s_ + 1) * 128, :], "cs%d" % s_)
                cast_next[0] += 1

        cast_upto(3)

        for e_ in ENGS:
            P.emit(e_, None, waits=ptoks, sig=False)

        stat_ctr = [0]

        def stat_cols(n):
            c0 = stat_ctr[0]
            if c0 + 3 * n > 256:
                c0 = 0
            stat_ctr[0] = c0 + 3 * n
            return stat[:, c0:c0 + n], stat[:, c0 + n:c0 + 2 * n], stat[:, c0 + 2 * n:c0 + 3 * n]

        def norm_stats(xaps, xbufs, junks, jbufs):
            n = len(xaps)
            ms, sd, rs = stat_cols(n)
            toks = []
            for i in range(n):
                f = (lambda x_, m_, j_: lambda e: e.activation(out=j_, in_=x_, func=AF.Square, scale=1.0 / 32.0, accum_out=m_))(
                    xaps[i], ms[:, i:i + 1], junks[i])
                tk = P.emit("act", f, waits=deps(reads=[xbufs[i]], writes=[jbufs[i]]))
                commit(tk, reads=[xbufs[i]], writes=[jbufs[i]])
                toks.append(tk)
            t2 = P.emit("act", lambda e: e.activation(out=sd, in_=ms, func=AF.Sqrt, bias=EPS, scale=1.0), waits=toks)
            t3 = P.emit("dve", lambda e: e.reciprocal(out=rs, in_=sd), waits=[t2])
            return rs, t3

        DEFER = [False]
        deferred = []

        def finish(fn):
            if DEFER[0]:
                deferred.append(fn)
            else:
                fn()

        def transpose_tile(hbi, t, hi, gcol0):
            b = acquire()
            mms = [(lambda k: lambda e: e.transpose(out=psum_b[b][:, k * 128:(k + 1) * 128], in_=hb[hbi][:, k * 128:(k + 1) * 128],
                                                    identity=ident))(k) for k in range(8)]
            mm_group(mms, [B_hb[hbi]], b)
            src = psum_b[b][:, 0:1024].rearrange("p (k t) -> p k t", k=8)
            dst = hTs[hi][:, :, t * 128:(t + 1) * 128]
            gb = bass.AP(arena, o_cvec // 4 + gcol0, [[AW, 128], [1, 8], [0, 128]])
            def fin():
                op("dve", lambda e: e.tensor_tensor(out=dst, in0=src, in1=gb, op=ALU.mult), reads=[B_ps[b], B_const], writes=[B_hTs[hi][t]])
                release(b)
            finish(fin)

        def fm_block(wv, wb, blk, evac, hi):
            b = acquire()
            wv3 = wv.rearrange("p (k c) -> p k c", k=8)
            mms = [MM(psum[b][:, :], wv3[:, k, blk * 128:(blk + 1) * 128], hTs[hi][:, k, :], k == 0, k == 7) for k in range(8)]
            mm_group(mms, [wb] + B_hTs[hi], b)

            def fin():
                evac(b)
                release(b)
            finish(fin)

        def xnorm_tasks(xd, c, hi, label):
            tasks = []
            st = {}
            for t in range(4):
                def ta_dma(t=t):
                    if t in st:
                        return
                    P.label = label
                    i = nxt("xt", 2)
                    st[t] = i
                    tk = P.dma("pool", xtmp[i], xd[c * T + t * 128:c * T + (t + 1) * 128, :], "xt%d" % i,
                               waits=deps(writes=[B_xtmp[i]]))
                    commit(tk, writes=[B_xtmp[i]])

                def ta(t=t, ta_dma=ta_dma):
                    ta_dma()
                    P.label = label
                    i = st[t]
                    rs, t3 = norm_stats([xtmp[i]], [B_xtmp[i]], [hb[i]], [B_hb[i]])
                    f = (lambda i_, r_: lambda e: e.tensor_scalar(out=hb[i_], in0=xtmp[i_], scalar1=r_, scalar2=None, op0=ALU.mult))(i, rs[:, 0:1])
                    tk = P.emit("dve", f, waits=[t3] + deps(reads=[B_xtmp[i]], writes=[B_hb[i]]))
                    commit(tk, reads=[B_xtmp[i]], writes=[B_hb[i]])
                ta.dma = ta_dma

                def tb(t=t):
                    P.label = label
                    transpose_tile(st[t], t, hi, 0)
                tasks += [ta, tb]
            return tasks

        cur_base = [0]

        def kv_tasks(xd, c, n, hi, base):
            s4, s5 = (c + base) % 4, (c + base) % 5
            hT = hTs[hi]
            B_hT = B_hTs[hi]
            tasks = xnorm_tasks(xd, c, hi, 'kv.norm')
            kblocks = [(0, 0), (0, 1), (0, 2), (0, 3), (1, 0), (1, 1), (1, 2)]
            cur = {}

            def slab(i):
                if i not in cur or wcontent.get(cur[i][2]) != cur[i][3]:
                    cur[i] = wget(S_KV0 + i)
                return cur[i][0], cur[i][1]

            for bi, (sl_, blk) in enumerate(kblocks):
                def tk_(bi=bi, sl_=sl_, blk=blk):
                    P.label = 'kv.K'
                    wv, wb = slab(sl_)
                    if bi < 5:
                        dst, dbuf = KTr[:, s4, bi, :], B_KT[s4]
                    else:
                        dst, dbuf = KT2r[:, s5, bi - 5, :], B_KT2[s5]

                    def evac(b):
                        tk = P.emit("act", lambda e: e.activation(out=dst, in_=psum[b][:, :], func=AF.Copy),
                                    waits=deps(reads=[B_ps[b]], writes=[dbuf]))
                        commit(tk, reads=[B_ps[b]])
                        dbuf.w.append(tk)

                    if bi == 0:
                        B_KT[s4].w, B_KT[s4].r = list(deps(writes=[B_KT[s4]])), []
                    if bi == 5:
                        B_KT2[s5].w, B_KT2[s5].r = list(deps(writes=[B_KT2[s5]])), []
                    fm_block(wv, wb, blk, evac, hi)
                tasks.append(tk_)

            def vtask(kind, idx):
                def tv():
                    P.label = 'kv.V'
                    if kind == 0:
                        wv, wb = slab(2)
                        wv3 = wv.rearrange("p (k c) -> p k c", k=8)
                        if idx == 0:
                            B_V[s4].w, B_V[s4].r = list(deps(writes=[B_V[s4]])), []
                        lw = [hT[:, k, idx * 128:(idx + 1) * 128] for k in range(8)]
                        rw = [wv3[:, k, 0:384] for k in range(8)]
                        ncol, dst, dbuf, rdh, eng = 384, Vr[:, s4, idx, 0:384], B_V[s4], [B_hT[idx]], "act"
                    elif kind == 1:
                        wv, wb = slab(3)
                        wv3 = wv.rearrange("p (k c) -> p k c", k=8)
                        lw = [hT[:, k, idx:512:4] for k in range(8)]
                        rw = [wv3[:, k, 0:256] for k in range(8)]
                        ncol, dst, dbuf, rdh, eng = 256, Vr[:, s4, idx, 384:640], B_V[s4], B_hT, "act"
                    else:
                        wv, wb = slab(3)
                        wv3 = wv.rearrange("p (k c) -> p k c", k=8)
                        if idx == 0:
                            B_V2[s5].w, B_V2[s5].r = list(deps(writes=[B_V2[s5]])), []
                        lw = [hT[:, k, idx:512:4] for k in range(8)]
                        rw = [wv3[:, k, 256:512] for k in range(8)]
                        ncol, dst, dbuf, rdh, eng = 256, V2r[:, s5, idx, :], B_V2[s5], B_hT, "act"
                    b = acquire()
                    mms = [MM(psum[b][:, 0:ncol], lw[k], rw[k], k == 0, k == 7) for k in range(8)]
                    mm_group(mms, [wb] + rdh, b)
                    if eng == "dve":
                        f = lambda e: e.tensor_copy(out=dst, in_=psum[b][:, 0:ncol])
                    else:
                        f = lambda e: e.activation(out=dst, in_=psum[b][:, 0:ncol], func=AF.Copy)
                    def fin():
                        tk = P.emit(eng, f, waits=deps(reads=[B_ps[b]], writes=[dbuf]))
                        commit(tk, reads=[B_ps[b]])
                        dbuf.w.append(tk)
                        release(b)
                    finish(fin)
                return tv

            vt = [[vtask(kind, idx) for idx in range(4)] for kind in range(3)]
            xn, kb = tasks[:8], tasks[8:]
            urgent = xn + kb[5:7] + vt[2]
            relaxed = kb[0:5] + vt[0] + vt[1]
            return urgent, relaxed

        def softmax_tile(bS, c0, c1, eb_ap, nbc=0):
            pi = nxt("PT", 8)
            src = psum[bS][:, c0:c1]
            op("act", lambda e: e.activation(out=src, in_=src, func=AF.Exp, scale=0.125), reads=[B_ps[bS]], writes=[B_ps[bS]])
            dstP = PTb[pi][:, c0:c1]
            if nbc:
                i0 = src.rearrange("p (r q) -> p r q", r=nbc)
                o0 = dstP.rearrange("p (r q) -> p r q", r=nbc)
                op("dve", lambda e: e.tensor_tensor(out=o0, in0=i0, in1=eb_ap, op=ALU.mult), reads=[B_ps[bS], B_eb], writes=[B_PT[pi]])
            else:
                op("dve", lambda e: e.tensor_tensor(out=dstP, in0=src, in1=eb_ap, op=ALU.mult), reads=[B_ps[bS], B_eb], writes=[B_PT[pi]])
            return pi

        class Job:
            __slots__ = ("s1", "s2", "s3", "pre3", "post3", "st")

            def __init__(self):
                self.pre3 = None
                self.post3 = None
                self.st = None

        def pv_emit(mo, rv, acc):
            bO, bD = acc["bO"], acc["bD"]
            w = deps(reads=rv + [B_const, B_ps[bO], B_ps[bD]])
            tok = None
            for i, f in enumerate(mo):
                tok = P.emit("pe", f, waits=w if i == 0 else (), sig=(i == len(mo) - 1))
            commit(tok, reads=rv)
            B_ps[bO].w = [tok]
            B_ps[bD].w = [tok]

        def banded_jobs(c, n, acc, hh, first, final, kt_blk, qt_blk, half, vcol, eb, dil):
            hs_ = slice(half * 64, half * 64 + 64)
            os_ = slice(hh * 64, hh * 64 + 64)
            cb = cur_base[0]
            jobs = []
            for t in range(4):
                if dil == 1:
                    Tg = 4 * c + t
                    keys = [(j, (Tg - 1 + j) // 4, (Tg - 1 + j) % 4) for j in range(3) if 0 <= Tg - 1 + j < 4 * n]
                    qsl = slice(t * 128, (t + 1) * 128)
                else:
                    keys = [(j, c - 1 + j, t) for j in range(3) if 0 <= c - 1 + j < n]
                    qsl = slice(t, 512, 4)
                jb = Job()

                def s1(keys=keys, qsl=qsl):
                    bS = acquire()
                    mms = []
                    rd_ = [B_QT[qt_blk]]
                    for (j, ck, tk) in keys:
                        ksl = slice(tk * 128, (tk + 1) * 128) if dil == 1 else slice(tk, 512, 4)
                        mms.append(MM(psum[bS][:, j * 128:(j + 1) * 128], KTr[hs_, (ck + cb) % 4, kt_blk, ksl], QT[hs_, qt_blk, qsl], True, True,
                                      tp=(half * 64, 0)))
                        rd_.append(B_KT[(ck + cb) % 4])
                    mm_group(mms, rd_, bS)
                    return bS

                def s2(bS, keys=keys):
                    j0, j1 = keys[0][0], keys[-1][0] + 1
                    pi = softmax_tile(bS, j0 * 128, j1 * 128, eb[:, j0 * 128:j1 * 128])
                    release(bS)
                    return pi

                def s3(pi, keys=keys, qsl=qsl):
                    bO, bD = acc["bO"], acc["bD"]
                    mo = []
                    rv = [B_PT[pi]]
                    for idx, (j, ck, tk) in enumerate(keys):
                        st = first and idx == 0
                        sp_ = final and idx == len(keys) - 1
                        rhs = PTb[pi][:, j * 128:(j + 1) * 128]
                        mo.append(MM(psum[bO][os_, qsl], Vr[:, (ck + cb) % 4, tk, vcol:vcol + 64], rhs, st, sp_, tp=(0, hh * 64)))
                        mo.append(MM(psum[bD][os_, qsl], ones[:, 0:64], rhs, st, sp_, tp=(0, hh * 64)))
                        rv.append(B_V[(ck + cb) % 4])
                    pv_emit(mo, rv, acc)

                jb.s1, jb.s2, jb.s3 = s1, s2, s3
                jobs.append(jb)
            return jobs

        def g2_jobs(c, n, acc, hh, pair, half, hs):
            hs_ = slice(half * 64, half * 64 + 64)
            os_ = slice(hh * 64, hh * 64 + 64)
            dls = [dl for dl in (-2, -1, 0, 1, 2) if 0 <= c + dl < n]
            cb = cur_base[0]
            jobs = []

            def rv4(ap2d, g4):
                return ap2d[:, g4:512:4]

            for dl in dls:
                ck = c + dl
                jb = Job()

                def s1(ck=ck):
                    bS = acquire()
                    mms = [MM(psum[bS][:, g4 * 128:(g4 + 1) * 128], rv4(KT2r[hs_, (ck + cb) % 5, pair, :], g4), rv4(QT[hs_, 8 + pair, :], g4),
                              True, True, tp=(half * 64, 0)) for g4 in range(4)]
                    mm_group(mms, [B_QT[8 + pair], B_KT2[(ck + cb) % 5]], bS)
                    return bS

                def s2(bS, dl=dl):
                    o2 = o_eb2 // 4 + ((dl + 2) * 4 + hs) * 128
                    eb = bass.AP(arena, o2, [[AW, 128], [0, 4], [1, 128]])
                    pi = softmax_tile(bS, 0, 512, eb, nbc=4)
                    release(bS)
                    return pi

                def s3(pi, ck=ck, dl=dl):
                    bO, bD = acc["bO"], acc["bD"]
                    last = dl == dls[-1]
                    mo = []
                    for g4 in range(4):
                        rhs = PTb[pi][:, g4 * 128:(g4 + 1) * 128]
                        mo.append(MM(rv4(psum[bO][os_, :], g4), V2r[:, (ck + cb) % 5, g4, hs * 64:hs * 64 + 64], rhs, False, last, tp=(0, hh * 64)))
                        mo.append(MM(rv4(psum[bD][os_, :], g4), ones[:, 0:64], rhs, False, last, tp=(0, hh * 64)))
                    pv_emit(mo, [B_PT[pi], B_V2[(ck + cb) % 5]], acc)

                jb.s1, jb.s2, jb.s3 = s1, s2, s3
                jobs.append(jb)
            return jobs

        def run_jobs(jobs, depth, fillers=None, G=2):
            fillers = fillers or {}
            groups = [jobs[i:i + G] for i in range(0, len(jobs), G)]
            ng = len(groups)
            fg = {}
            pend = {}
            for st_, fs in fillers.items():
                fg.setdefault(st_ // G, []).extend(fs)
            for gi in range(ng + depth):
                if gi < ng:
                    for jb in groups[gi]:
                        jb.st = jb.s2(jb.s1())
                k = gi - depth
                if k >= 0:
                    for jb in groups[k]:
                        if jb.pre3 is not None:
                            jb.pre3()
                        jb.s3(jb.st)
                        if jb.post3 is not None:
                            jb.post3()
                for fn in pend.pop(gi, ()):
                    fn()
                DEFER[0] = True
                for f_ in fg.get(gi, ()):
                    f_()
                    P.label = 'm.att'
                DEFER[0] = False
                if deferred:
                    pend.setdefault(gi + 2, []).extend(deferred)
                    del deferred[:]
            for gi in sorted(pend):
                for fn in pend[gi]:
                    fn()
            for gi in sorted(fg):
                if gi >= ng + depth:
                    for f_ in fg[gi]:
                        f_()

        def normalize(bO, bD, dst, dbuf, sink_col):
            ri = nxt("rd", 2)
            rd = rdb[ri]
            if sink_col is not None:
                op("dve", lambda e: e.tensor_scalar(out=rd, in0=psum[bD][:, :], scalar1=sink_col, scalar2=None, op0=ALU.add),
                   reads=[B_ps[bD], B_const], writes=[B_rd[ri]])
                op("dve", lambda e: e.reciprocal(out=rd, in_=rd), reads=[B_rd[ri]], writes=[B_rd[ri]])
            else:
                op("dve", lambda e: e.reciprocal(out=rd, in_=psum[bD][:, :]), reads=[B_ps[bD]], writes=[B_rd[ri]])
            op("dve", lambda e: e.tensor_tensor(out=dst, in0=psum[bO][:, :], in1=rd, op=ALU.mult), reads=[B_ps[bO], B_rd[ri]], writes=[dbuf])

        def main_stage(xd, yd, c, n, hi, prefetched, kvf, nextx, base):
            hT = hTs[hi]
            B_hT = B_hTs[hi]
            cur_base[0] = base
            P.label = 'm.norm1'
            if not prefetched:
                for f_ in xnorm_tasks(xd, c, hi, 'm.norm1'):
                    f_()
            if stop == 'load':
                return
            kvf = (list(kvf[0]), list(kvf[1])) if kvf else None
            if kvf:
                kvf[0][0].dma()
                kvf[0][2].dma()
            P.label = 'm.Q'
            qb = 0
            for sl_, nb in ((0, 4), (1, 4), (2, 2)):
                wv, wb, _, _ = wget(S_Q0 + sl_)
                for blk in range(nb):
                    def evac(b, qb=qb):
                        op("act", lambda e: e.activation(out=QT[:, qb, :], in_=psum[b][:, :], func=AF.Copy), reads=[B_ps[b]], writes=[B_QT[qb]])
                    fm_block(wv, wb, blk, evac, hi)
                    qb += 1
            if stop == 'q':
                return
            P.label = 'm.att'
            jobs = []
            for pair in range(4):
                acc = {}
                pj = []
                for hh in range(2):
                    h = 2 * pair + hh
                    pj += banded_jobs(c, n, acc, hh, True, True, 0, h % 4, h // 4, (h // 4) * 64, ebA[:, h, :], 1)

                def preA(acc=acc):
                    bO, bD = acquire(), acquire()
                    acc["bO"], acc["bD"] = bO, bD
                    B_ps[bO].w, B_ps[bO].r = list(deps(writes=[B_ps[bO]])), []
                    B_ps[bD].w, B_ps[bD].r = list(deps(writes=[B_ps[bD]])), []

                def postA(acc=acc, pair=pair):
                    normalize(acc["bO"], acc["bD"], oTA[:, pair, :], B_oTA[pair], esink[:, pair:pair + 1])
                    release(acc["bO"])
                    release(acc["bD"])

                pj[0].pre3 = preA
                pj[-1].post3 = postA
                jobs += pj
            for pair in range(2):
                acc = {}
                pj = []
                for hh in range(2):
                    hs = 2 * pair + hh
                    pj += banded_jobs(c, n, acc, hh, False, False, 1 + pair, 4 + pair, hh, 128 + hs * 64, ebB[:, 0, hs, :], 1)
                    pj += banded_jobs(c, n, acc, hh, False, False, 3 + pair, 6 + pair, hh, 384 + hs * 64, ebB[:, 1, hs, :], 4)
                    pj += g2_jobs(c, n, acc, hh, pair, hh, hs)

                def preB(acc=acc):
                    bO, bD = acquire(), acquire()
                    acc["bO"], acc["bD"] = bO, bD
                    mm_group([MM(psum[bO][:, :], zeros[:, 0:128], zeros[:, :], True, True)], [B_const], bO)
                    mm_group([MM(psum[bD][:, :], zeros[:, 0:128], zeros[:, :], True, True)], [B_const], bD)

                def postB(acc=acc, pair=pair):
                    normalize(acc["bO"], acc["bD"], oTB[:, pair, :], B_oTB[pair], None)
                    release(acc["bO"])
                    release(acc["bD"])

                pj[0].pre3 = preB
                pj[-1].post3 = postB
                jobs += pj
            urgent, relaxed = kvf if kvf else ([], [])
            late = relaxed[len(relaxed) - 8:] if relaxed else []
            early = relaxed[:len(relaxed) - len(late)]
            fl = {}
            if len(urgent) == 14:
                order = [0, 2, 1, 4, 3, 6, 5, 7, 8, 9, 10, 11, 12, 13]
                steps = [1, 2, 6, 7, 10, 11, 14, 17, 23, 25, 27, 29, 31, 33]
                for o_, st_ in zip(order, steps):
                    fl.setdefault(st_, []).append(urgent[o_])
            else:
                for q_, f_ in enumerate(urgent):
                    fl.setdefault(1 + 2 * q_, []).append(f_)
            for q_, f_ in enumerate(early):
                fl.setdefault(36 + 5 * q_, []).append(f_)
            run_jobs(jobs, 3, fl, G=2)
            if stop == 'attB':
                return
            if dbg_c == c:
                for nm, view, bufs in (("d_QT", VB(o_R, 5120), B_QT), ("d_oTA", VB(o_R + 10240, 2048), B_oTA), ("d_oTB", VB(o_R + 18432, 1024), B_oTB),
                                       ("d_hT", VB(o_hT, 4096), B_hT)):
                    tkd = P.dma("pool", dbgt[nm], view, nm, waits=deps(reads=bufs))
                    commit(tkd, reads=bufs + B_mT + B_aT)
                    for e_ in ENGS:
                        P.emit(e_, None, waits=[tkd], sig=False)
            P.label = 'm.merge'
            for t in range(4):
                tk = P.dma("pool", xres[t], xd[c * T + t * 128:c * T + (t + 1) * 128, :], "xr%d" % t, waits=deps(writes=[B_xres[t]]))
                commit(tk, writes=[B_xres[t]])
            def merge_block(j):
                wv, wb, _, _ = wget(S_M0 + j)
                wg = wv[:, 0:2048].rearrange("p (k c) -> p k c", k=8)
                wa = wv[:, 2048:2560].rearrange("p (h c) -> p h c", h=4)
                wbb = wv[:, 2560:2816].rearrange("p (h c) -> p h c", h=2)
                bG = [acquire(), acquire()]
                for gi in range(2):
                    mms = [MM(psum[bG[gi]][:, :], wg[:, k, gi * 128:(gi + 1) * 128], hT[:, k, :], k == 0, k == 7) for k in range(8)]
                    mm_group(mms, [wb] + B_hT, bG[gi])
                bYa, bYb = acquire(), acquire()
                mms = [MM(psum[bYa][:, :], wa[:, p_, :], oTA[:, p_, :], p_ == 0, p_ == 3) for p_ in range(4)]
                mm_group(mms, [wb] + B_oTA, bYa)
                mms = [MM(psum[bYb][:, :], wbb[:, p_, :], oTB[:, p_, :], p_ == 0, p_ == 1) for p_ in range(2)]
                mm_group(mms, [wb] + B_oTB, bYb)
                sis = []
                for gi in range(2):
                    si = nxt("sg", 4)
                    sis.append(si)
                    bcol = bg[:, gi * 8 + j:gi * 8 + j + 1]
                    op("act", (lambda si_, b_, bc: lambda e: e.activation(out=sgb[si_], in_=psum[b_][:, :], func=AF.Sigmoid, bias=bc, scale=1.0))(
                        si, bG[gi], bcol), reads=[B_ps[bG[gi]], B_const], writes=[B_sg[si]])
                    release(bG[gi])
                op("dve", lambda e: e.tensor_tensor(out=sgb[sis[0]], in0=psum[bYa][:, :], in1=sgb[sis[0]], op=ALU.mult),
                   reads=[B_ps[bYa], B_sg[sis[0]]], writes=[B_sg[sis[0]]])
                op("dve", lambda e: e.tensor_tensor(out=sgb[sis[1]], in0=psum[bYb][:, :], in1=sgb[sis[1]], op=ALU.mult),
                   reads=[B_ps[bYb], B_sg[sis[1]]], writes=[B_sg[sis[1]]])
                release(bYa)
                release(bYb)
                op("dve" if j == 7 else "pool", lambda e: e.tensor_tensor(out=mergedT[:, j, :], in0=sgb[sis[0]], in1=sgb[sis[1]], op=ALU.add),
                   reads=[B_sg[sis[0]], B_sg[sis[1]]], writes=[B_mT[j]])

            for j in range(8):
                merge_block(j)
            if stop == 'merge':
                return
            if dbg_c == c:
                tkd = P.dma("pool", dbgt["d_mT"], VB(o_R, 4096), "d_mT", waits=deps(reads=B_mT))
                commit(tkd, reads=B_mT + B_aT)
                for e_ in ENGS:
                    P.emit(e_, None, waits=[tkd], sig=False)
            P.label = 'm.out'
            for half in range(2):
                wv, wb, _, _ = wget(S_O0 + half)
                wv3 = wv.rearrange("p (k c) -> p k c", k=8)
                for t in range(4):
                    b = acquire()
                    mms = [MM(psum[b][:, :], mergedT[:, k, t * 128:(t + 1) * 128], wv3[:, k, :], k == 0, k == 7) for k in range(8)]
                    mm_group(mms, [wb] + B_mT, b)
                    xs_ = xres[t][:, half * 512:(half + 1) * 512]
                    op("dve", (lambda x_, b_: lambda e: e.tensor_tensor(out=x_, in0=psum[b_][:, :], in1=x_, op=ALU.add))(xs_, b),
                       reads=[B_ps[b], B_xres[t]], writes=[B_xres[t]])
                    release(b)
            if stop == 'out':
                return
            if dbg_c == c:
                tkd = P.dma("pool", dbgt["d_x1"], VF(o_xres, 4096), "d_x1", waits=deps(reads=B_xres))
                commit(tkd, reads=B_xres)
                for e_ in ENGS:
                    P.emit(e_, None, waits=[tkd], sig=False)
            P.label = 'm.norm2'
            his = []
            for t in range(4):
                i = nxt("xt", 2)
                his.append(i)
                rs, t3 = norm_stats([xres[t]], [B_xres[t]], [hb[i]], [B_hb[i]])
                f = (lambda i_, t_, r_: lambda e: e.tensor_scalar(out=hb[i_], in0=xres[t_], scalar1=r_, scalar2=None, op0=ALU.mult))(
                    i, t, rs[:, 0:1])
                tk = P.emit("dve", f, waits=[t3] + deps(reads=[B_xres[t]], writes=[B_hb[i]]))
                commit(tk, reads=[B_xres[t]], writes=[B_hb[i]])
                if t < 2:
                    for f_ in late[t * 4:(t + 1) * 4]:
                        f_()
                    P.label = 'm.norm2'
                if t >= 1:
                    transpose_tile(his[t - 1], t - 1, hi, 8)
            transpose_tile(his[3], 3, hi, 8)
            if stop == 'norm2':
                return
            P.label = 'm.up'
            pre = xnorm_tasks(nextx[0], nextx[1], 1 - hi, 'm.pre') if nextx is not None else []
            if pre:
                pre = [pre[i_] for i_ in (0, 2, 1, 4, 3, 6, 5, 7)]
            for ui in range(8):
                wv, wb, _, _ = wget(S_U0 + ui)
                for blk in range(4):
                    fb = ui * 4 + blk

                    def evac(b, fb=fb):
                        ri = nxt("rel", 2)
                        op("act", lambda e: e.activation(out=relb[ri], in_=psum[b][:, :], func=AF.Relu), reads=[B_ps[b]], writes=[B_rel[ri]])
                        op("dve", lambda e: e.tensor_tensor(out=aT[:, fb, :], in0=relb[ri], in1=relb[ri], op=ALU.mult),
                           reads=[B_rel[ri]], writes=[B_aT[fb]])
                    fm_block(wv, wb, blk, evac, hi)
                    if pre and fb % 3 == 2:
                        pre.pop(0)()
                        P.label = 'm.up'
            while pre:
                pre.pop(0)()
            if stop == 'up':
                return
            if dbg_c == c:
                tkd = P.dma("pool", dbgt["d_aT"], VB(o_R, 16384), "d_aT", waits=deps(reads=B_aT))
                commit(tkd, reads=B_aT)
                for e_ in ENGS:
                    P.emit(e_, None, waits=[tkd], sig=False)
            P.label = 'm.down'
            for half in range(2):
                bt = [acquire() for _ in range(4)]
                for t in range(4):
                    B_ps[bt[t]].w, B_ps[bt[t]].r = list(deps(writes=[B_ps[bt[t]]])), []
                for kg in range(4):
                    wv, wb, _, _ = wget(S_D0 + half * 4 + kg)
                    wv3 = wv.rearrange("p (k c) -> p k c", k=8)
                    mms = []
                    for kk in range(8):
                        fb = kg * 8 + kk
                        for t in range(4):
                            mms.append(MM(psum[bt[t]][:, :], aT[:, fb, t * 128:(t + 1) * 128], wv3[:, kk, :], fb == 0, fb == 31))
                    w = deps(reads=[wb] + B_aT[kg * 8:kg * 8 + 8] + [B_ps[bt[t]] for t in range(4)])
                    tok = None
                    for i, f in enumerate(mms):
                        tok = P.emit("pe", f, waits=w if i == 0 else (), sig=(i == len(mms) - 1))
                    commit(tok, reads=[wb] + B_aT[kg * 8:kg * 8 + 8])
                    for t in range(4):
                        B_ps[bt[t]].w = [tok]
                for t in range(4):
                    xs_ = xres[t][:, half * 512:(half + 1) * 512]
                    op("dve", (lambda x_, b_: lambda e: e.tensor_tensor(out=x_, in0=psum[b_][:, :], in1=x_, op=ALU.add))(xs_, bt[t]),
                       reads=[B_ps[bt[t]], B_xres[t]], writes=[B_xres[t]])
                    release(bt[t])
            if stop == 'down':
                return
            P.label = 'm.final'
            rs, t3 = norm_stats(xres, B_xres, [hb[t % 2] for t in range(4)], [B_hb[t % 2] for t in range(4)])
            for t in range(4):
                f = (lambda t_, r_: lambda e: e.scalar_tensor_tensor(out=xres[t_], in0=xres[t_], scalar=r_, in1=gfin, op0=ALU.mult, op1=ALU.mult))(
                    t, rs[:, t:t + 1])
                tk = P.emit("dve", f, waits=[t3] + deps(reads=[B_const], writes=[B_xres[t]]))
                commit(tk, writes=[B_xres[t]])
                ty = P.dma("pool", yd[c * T + t * 128:c * T + (t + 1) * 128, :], xres[t], "ys%d" % t, waits=[tk])
                commit(ty, reads=[B_xres[t]])

        sq = []
        base_ = 0
        for name, yname, S in seqs:
            assert (S // T) % 2 == 0
            sq.append((dr[name], dr[yname], S // T, base_))
            base_ += S // T
        for si, (xd, yd, n, base) in enumerate(sq):
            if stop == "pro":
                break
            nxt_seq = sq[si + 1] if si + 1 < len(sq) else None
            if si == 0:
                for cc in range(min(2, n)):
                    u_, r_ = kv_tasks(xd, cc, n, 1, base)
                    for f_ in u_ + r_:
                        f_()
                    if stop == "kv":
                        break
                if stop == "kv":
                    break
            for c in range(n):
                hi = c % 2
                if c + 2 < n:
                    kvf = kv_tasks(xd, c + 2, n, 1 - hi, base)
                elif nxt_seq is not None:
                    kvf = kv_tasks(nxt_seq[0], c + 2 - n, nxt_seq[2], 1 - hi, nxt_seq[3])
                else:
                    kvf = None
                if c + 1 < n:
                    nextx = (xd, c + 1)
                elif nxt_seq is not None:
                    nextx = (nxt_seq[0], 0)
                else:
                    nextx = None
                main_stage(xd, yd, c, n, hi, not (si == 0 and c == 0), kvf, nextx, base)
                if stop is not None:
                    break
            if stop is not None:
                break
        fin = [("ys%d" % t, P.cnt["ys%d" % t]) for t in range(4)] + [(nm, P.cnt[nm]) for nm in ("d_QT", "d_oTA", "d_oTB", "d_hT", "d_mT", "d_x1", "d_aT")]
        P.emit("pool", None, waits=fin, sig=False)
        P.emit("sp", None, waits=fin, sig=False)

        import os as _os
        if _os.environ.get("KPROF"):
            import json as _json
            _json.dump({"semorder": P.semorder, "tokmap": [[k[0], k[1], v] for k, v in P.tokmap.items()]}, open(_os.environ["KPROF"], "w"))
        block = es.enter_context(nc.Block())

        @block.sync
        def _(e):
            for f in P.ops["sp"]:
                f(e)

        @block.gpsimd
        def _(e):
            for f in P.ops["pool"]:
                f(e)

        @block.scalar
        def _(e):
            for f in P.ops["act"]:
                f(e)

        @block.vector
        def _(e):
            for f in P.ops["dve"]:
                f(e)

        @block.tensor
        def _(e):
            for f in P.ops["pe"]:
                f(e)
    return nc


_SEQS = [("xp", "yp", SEQ_P), ("xs", "ys", SEQ_S)]


def kernel(x_prompt, x_sample, rel_bias, g_mix, w_in, b_gate, w_branch_a, w_branch_b, w_out,
           attn_sink, g_mlp, w_up, w_down, g_final):
    f = lambda a: np.ascontiguousarray(np.asarray(a, dtype=np.float32))
    x_prompt, x_sample = f(x_prompt), f(x_sample)
    tabA, tabB, tab2 = _bias_tables(f(rel_bias))
    wsl = _slabs(f(w_in)[0], f(w_branch_a)[0], f(w_branch_b)[0], f(w_out)[0], f(w_up)[0], f(w_down)[0])
    cvec = np.zeros((128, 64), np.float32)
    cvec[:, 0:8] = f(g_mix)[0].reshape(8, 128).T
    cvec[:, 8:16] = f(g_mlp)[0].reshape(8, 128).T
    cvec[:, 16:32] = f(b_gate)[0].reshape(16, 128).T
    sk = f(attn_sink)[0]
    for pair in range(4):
        cvec[0:64, 32 + pair] = sk[2 * pair]
        cvec[64:128, 32 + pair] = sk[2 * pair + 1]
    gfin = np.ascontiguousarray(np.broadcast_to(f(g_final)[None, :], (128, D)))
    nc = build_program(_SEQS)
    in_maps = []
    for i in range(NCORES):
        in_maps.append({"xp": x_prompt[i], "xs": x_sample[i], "wsl": wsl, "tabA": tabA, "tabB": tabB, "tab2": tab2,
                        "cvec": cvec, "gfin": gfin})
    res = run_bass_kernel_spmd(nc, in_maps, core_ids=list(range(NCORES)))
    yp = np.stack([np.asarray(r["yp"], dtype=np.float32) for r in res.results], axis=0)
    ys = np.stack([np.asarray(r["ys"], dtype=np.float32) for r in res.results], axis=0)
    return (yp, ys)
```

```python
import numpy as np
from contextlib import ExitStack
import concourse.bass as bass
import concourse.mybir as mybir
from concourse.bass_utils import run_bass_kernel_spmd

F32 = mybir.dt.float32
BF16 = mybir.dt.bfloat16
AF = mybir.ActivationFunctionType
ALU = mybir.AluOpType

D = 1024
T = 512
NCORES = 8
SEQ_P = 2048
SEQ_S = 8192
NSLAB = 33
S_KV0, S_Q0, S_M0, S_O0, S_U0, S_D0 = 0, 4, 7, 15, 17, 25
NW = 3
EPS = 1e-6
MASKV = -30000.0


def _rel_bucket(rel):
    half, max_exact = 16, 8
    n = np.abs(rel)
    large = max_exact + (np.log(np.maximum(n, 1) / max_exact) / np.log(1024 / max_exact) * (half - max_exact)).astype(np.int32)
    large = np.minimum(large, half - 1)
    return (np.where(rel > 0, half, 0) + np.where(n < max_exact, n, large)).astype(np.int32)


def _bias_tables(rel_bias):
    rb = np.concatenate([rel_bias.astype(np.float32), np.full((1, rel_bias.shape[1]), MASKV, np.float32)], axis=0)
    k = np.arange(128)[:, None, None]
    j = np.arange(3)[None, :, None]
    q = np.arange(128)[None, None, :]
    rel = (j - 1) * 128 + k - q

    def tab(nwin, scale, col0, nh):
        idx = np.where(np.abs(rel) <= nwin, _rel_bucket(rel * scale), 32)
        out = np.stack([rb[idx, col0 + h] for h in range(nh)], axis=1)
        return np.ascontiguousarray(out.reshape(128, -1))

    tabA = tab(128, 1, 0, 8)
    tabg0 = tab(64, 1, 8, 4)
    tabg1 = tab(64, 4, 12, 4)
    rl = np.arange(128) % 4
    jj = np.arange(128) // 4
    idx2 = np.full((128, 5, 128), 32, np.int64)
    for di, dl in enumerate((-2, -1, 0, 1, 2)):
        r2 = 32 * dl + jj[:, None] - jj[None, :]
        ok = (rl[:, None] == rl[None, :]) & (np.abs(r2) <= 64)
        idx2[:, di, :] = np.where(ok, _rel_bucket(r2 * 16), 32)
    tab2 = np.stack([rb[idx2, 16 + h] for h in range(4)], axis=2)
    tab2 = np.ascontiguousarray(tab2.reshape(128, -1))
    return tabA, np.concatenate([tabg0, tabg1], axis=1), tab2


def _pkc(w):
    K = w.shape[0] // 128
    return np.ascontiguousarray(w.reshape(K, 128, -1).transpose(1, 0, 2)).reshape(128, -1)


def _slabs(w_in, w_a, w_b, w_out, w_up, w_down):
    sl = np.zeros((NSLAB, 128, 4096), np.float32)

    def put_fm(s, cols):
        sub = np.zeros((1024, 512), np.float32)
        sub[:, :len(cols)] = w_in[:, cols]
        sl[s] = _pkc(sub)

    ar = np.arange
    kcols = np.concatenate([ar(512, 640), ar(1536, 2304)])
    put_fm(S_KV0 + 0, kcols[:512])
    put_fm(S_KV0 + 1, kcols[512:])
    put_fm(S_KV0 + 2, np.concatenate([ar(640, 768), ar(2304, 2560)]))
    put_fm(S_KV0 + 3, ar(2560, 3072))
    qa = np.concatenate([np.concatenate([ar(i * 64, i * 64 + 64), ar((i + 4) * 64, (i + 4) * 64 + 64)]) for i in range(4)])
    put_fm(S_Q0 + 0, qa)
    put_fm(S_Q0 + 1, ar(768, 1280))
    put_fm(S_Q0 + 2, ar(1280, 1536))
    for j in range(8):
        g = np.zeros((1024, 256), np.float32)
        g[:, :128] = w_in[:, 3072 + j * 128:3072 + j * 128 + 128]
        g[:, 128:] = w_in[:, 4096 + j * 128:4096 + j * 128 + 128]
        sl[S_M0 + j, :, 0:2048] = _pkc(g)
        sl[S_M0 + j, :, 2048:2560] = _pkc(w_a[:, j * 128:j * 128 + 128])
        sl[S_M0 + j, :, 2560:2816] = _pkc(w_b[:, j * 128:j * 128 + 128])
    for h in range(2):
        sl[S_O0 + h] = _pkc(w_out[:, h * 512:h * 512 + 512])
    for i in range(8):
        sl[S_U0 + i] = _pkc(w_up[:, i * 512:i * 512 + 512])
    for h in range(2):
        for kg in range(4):
            sl[S_D0 + h * 4 + kg] = _pkc(w_down[kg * 1024:(kg + 1) * 1024, h * 512:h * 512 + 512])
    return sl.reshape(NSLAB * 128, 4096)


ENGS = ("pe", "act", "dve", "pool", "sp")


class Buf:
    __slots__ = ("w", "r")

    def __init__(self):
        self.w = []
        self.r = []


class Planner:
    def __init__(self, nc, es):
        self.nc = nc
        self.es = es
        self.ops = {e: [] for e in ENGS}
        self.sem = {}
        self.cnt = {}
        self.waited = {e: {} for e in ENGS}
        self.label = ""
        self.tokmap = {}
        self.semorder = []
        for e in ("pe", "act", "dve", "pool"):
            self.newsem(e)

    def newsem(self, name):
        self.sem[name] = self.es.enter_context(self.nc.semaphore("s_" + name))
        self.cnt[name] = 0
        self.semorder.append(name)

    def _waits(self, eng, toks):
        best = {}
        for tk in toks:
            if tk is None:
                continue
            s, v = tk
            if eng == "pe" and s == "pe":
                continue
            if self.waited[eng].get(s, 0) >= v:
                continue
            if best.get(s, 0) < v:
                best[s] = v
        for s, v in best.items():
            self.waited[eng][s] = v
        return list(best.items())

    def emit(self, eng, fn, waits=(), sig=True, force=None):
        wl = self._waits(eng, waits)
        if force is not None:
            wl = wl + [force]
        tok = None
        if sig:
            self.cnt[eng] += 1
            tok = (eng, self.cnt[eng])
            self.tokmap[tok] = self.label
        sems = self.sem

        def run(e):
            for s, v in wl:
                e.wait_ge(sems[s], v)
            if fn is not None:
                ins = fn(e)
                if sig:
                    ins.then_inc(sems[eng], 1)

        self.ops[eng].append(run)
        return tok

    def dma(self, q, out, in_, semname, waits=()):
        wl = self._waits(q, waits)
        self.cnt[semname] += 16
        tok = (semname, self.cnt[semname])
        self.tokmap[tok] = self.label
        sems = self.sem

        def run(e):
            for s, v in wl:
                e.wait_ge(sems[s], v)
            e.dma_start(out=out, in_=in_).then_inc(sems[semname], 16)

        self.ops[q].append(run)
        return tok


def deps(reads=(), writes=()):
    t = []
    for b in reads:
        t += b.w
    for b in writes:
        t += b.w
        t += b.r
    return t


def commit(tok, reads=(), writes=()):
    for b in reads:
        b.r.append(tok)
        if len(b.r) > 8:
            best = {}
            for s, v in b.r:
                if best.get(s, 0) < v:
                    best[s] = v
            b.r = list(best.items())
    for b in writes:
        b.w = [tok]
        b.r = []


def build_program(seqs, dbg_c=None, stop=None):
    nc = bass.Bass("TRN2", target_bir_lowering=False)
    dr = {}
    dbgt = {}
    if dbg_c is not None:
        for nm, w_ in (("d_QT", 5120), ("d_oTA", 2048), ("d_oTB", 1024), ("d_mT", 4096), ("d_hT", 4096), ("d_aT", 16384)):
            dbgt[nm] = nc.dram_tensor(nm, [128, w_], BF16, kind="ExternalOutput").ap()
        dbgt["d_x1"] = nc.dram_tensor("d_x1", [128, 4096], F32, kind="ExternalOutput").ap()
    for name, yname, S in seqs:
        dr[name] = nc.dram_tensor(name, [S, D], F32, kind="ExternalInput").ap()
        dr[yname] = nc.dram_tensor(yname, [S, D], F32, kind="ExternalOutput").ap()
    wsl = nc.dram_tensor("wsl", [NSLAB * 128, 4096], F32, kind="ExternalInput").ap()
    tabA_d = nc.dram_tensor("tabA", [128, 8 * 3 * 128], F32, kind="ExternalInput").ap()
    tabB_d = nc.dram_tensor("tabB", [128, 2 * 4 * 3 * 128], F32, kind="ExternalInput").ap()
    tab2_d = nc.dram_tensor("tab2", [128, 5 * 4 * 128], F32, kind="ExternalInput").ap()
    cvec_d = nc.dram_tensor("cvec", [128, 64], F32, kind="ExternalInput").ap()
    gfin_d = nc.dram_tensor("gfin", [128, D], F32, kind="ExternalInput").ap()
    wscr = nc.dram_tensor("wscr", [NSLAB * 128, 4096], BF16, kind="Internal").ap()

    es = ExitStack()
    with es:
        off = [0]

        def A(nbytes):
            o = off[0]
            off[0] += (nbytes + 31) // 32 * 32
            return o

        o_ident = A(256)
        o_ones = A(128)
        o_zero = A(1024)
        o_ebA = A(8 * 3 * 128 * 4)
        o_ebB = A(2 * 4 * 3 * 128 * 4)
        o_eb2 = A(5 * 4 * 128 * 4)
        o_gfin = A(4096)
        o_cvec = A(256)
        o_esink = A(32)
        o_stat = A(1024)
        o_identf = A(512)
        o_ktr = A(4 * 5 * 512 * 2)
        o_kt2 = A(5 * 2 * 512 * 2)
        o_vr = A(4 * 4 * 640 * 2)
        o_v2 = A(5 * 4 * 256 * 2)
        o_xtmp = A(2 * 4096)
        o_xres = A(16384)
        o_hb = A(2 * 2048)
        o_hT = A(2 * 8192)
        o_R = A(36864)
        o_w = A(NW * 8192)
        total = off[0]
        AW = total // 4
        arena = es.enter_context(nc.sbuf_tensor("arena", [128, AW], F32))
        arena_b = arena.bitcast(BF16)

        def VF(o, n):
            return arena[:, o // 4:o // 4 + n]

        def VB(o, n):
            return arena_b[:, o // 2:o // 2 + n]

        ident = VB(o_ident, 128)
        ones = VB(o_ones, 64)
        zeros = VB(o_zero, 512)
        identf = VF(o_identf, 128)
        ebA = VF(o_ebA, 8 * 3 * 128).rearrange("p (h x) -> p h x", h=8)
        ebB = VF(o_ebB, 2 * 4 * 3 * 128).rearrange("p (g h x) -> p g h x", g=2, h=4)
        gfin = VF(o_gfin, 1024)
        cvec = VF(o_cvec, 64)
        gm = cvec[:, 0:8]
        gl = cvec[:, 8:16]
        bg = cvec[:, 16:32]
        sinkp = cvec[:, 32:36]
        esink = VF(o_esink, 4)
        stat = VF(o_stat, 256)
        KTr = VB(o_ktr, 4 * 5 * 512).rearrange("p (s b t) -> p s b t", s=4, b=5)
        KT2r = VB(o_kt2, 5 * 2 * 512).rearrange("p (s b t) -> p s b t", s=5, b=2)
        Vr = VB(o_vr, 4 * 4 * 640).rearrange("p (s t c) -> p s t c", s=4, t=4)
        V2r = VB(o_v2, 5 * 4 * 256).rearrange("p (s r c) -> p s r c", s=5, r=4)
        xtmp = [VF(o_xtmp + i * 4096, 1024) for i in range(2)]
        xres = [VF(o_xres + t * 4096, 1024) for t in range(4)]
        hb = [VB(o_hb + i * 2048, 1024) for i in range(2)]
        hTs = [VB(o_hT + i * 8192, 4096).rearrange("p (k t) -> p k t", k=8) for i in range(2)]
        QT = VB(o_R, 5120).rearrange("p (b t) -> p b t", b=10)
        mergedT = VB(o_R, 4096).rearrange("p (b t) -> p b t", b=8)
        oTA = VB(o_R + 10240, 2048).rearrange("p (b t) -> p b t", b=4)
        oTB = VB(o_R + 18432, 1024).rearrange("p (b t) -> p b t", b=2)
        Ebuf = [VF(o_R + 22528 + i * 2048, 512) for i in range(2)]
        PTb = [VB(o_R + 22528 + i * 1024, 512) for i in range(8)]
        rdb = [VF(o_R + 30720 + i * 2048, 512) for i in range(2)]
        sgb = [VF(o_R + 22528 + i * 2048, 512) for i in range(4)]
        tpb = [VF(o_R + 30720 + i * 2048, 512) for i in range(2)]
        aT = VB(o_R, 16384).rearrange("p (b t) -> p b t", b=32)
        relb = [VF(o_R + 32768 + i * 2048, 512) for i in range(2)]
        wslot = [VB(o_w + i * 8192, 4096) for i in range(NW)]
        stgA = VF(o_xres, 4096)
        stgB = VF(o_R, 4096)

        psum = [es.enter_context(nc.psum_tensor("ps%d" % i, [128, 512], F32)) for i in range(8)]
        psum_b = [p.bitcast(BF16) for p in psum]

        P = Planner(nc, es)
        for i in range(NW):
            P.newsem("w%d" % i)
        for i in range(2):
            P.newsem("xt%d" % i)
        for t in range(4):
            P.newsem("xr%d" % t)
            P.newsem("ys%d" % t)
        for s_ in range(NSLAB):
            P.newsem("cs%d" % s_)
        for nm in ("tb", "tb2", "d_QT", "d_oTA", "d_oTB", "d_hT", "d_mT", "d_x1", "d_aT"):
            P.newsem(nm)

        B_ps = [Buf() for _ in range(8)]
        B_w = [Buf() for _ in range(NW)]
        B_xtmp = [Buf(), Buf()]
        B_xres = [Buf() for _ in range(4)]
        B_hb = [Buf(), Buf()]
        B_hTs = [[Buf() for _ in range(4)] for _ in range(2)]
        B_KT = [Buf() for _ in range(4)]
        B_KT2 = [Buf() for _ in range(5)]
        B_V = [Buf() for _ in range(4)]
        B_V2 = [Buf() for _ in range(5)]
        B_QT = [Buf() for _ in range(10)]
        B_E = [Buf(), Buf()]
        B_PT = [Buf() for _ in range(8)]
        B_rd = [Buf(), Buf()]
        B_oTA = [Buf() for _ in range(4)]
        B_oTB = [Buf(), Buf()]
        B_sg = [Buf() for _ in range(4)]
        B_tp = [Buf(), Buf()]
        B_mT = [Buf() for _ in range(8)]
        B_aT = [Buf() for _ in range(32)]
        B_rel = [Buf(), Buf()]
        B_stg = [Buf(), Buf()]
        B_const = Buf()
        B_eb = Buf()
        B_junk = Buf()

        free_banks = list(range(8))

        def acquire():
            assert free_banks, "out of PSUM banks"
            return free_banks.pop(0)

        def release(b):
            free_banks.append(b)

        rr = {"E": 0, "PT": 0, "rd": 0, "sg": 0, "tp": 0, "rel": 0, "xt": 0, "st": 0, "w": 0}

        def nxt(key, n):
            v = rr[key]
            rr[key] = (v + 1) % n
            return v

        wcontent = {}
        wserial = [0]

        def wget(slab):
            i = nxt("w", NW)
            wserial[0] += 1
            wcontent[i] = wserial[0]
            cast_upto(slab + 6)
            tok = P.dma("sp", wslot[i], wscr[slab * 128:(slab + 1) * 128, :], "w%d" % i, waits=deps(writes=[B_w[i]]) + [cast_tok[slab]])
            commit(tok, writes=[B_w[i]])
            return wslot[i], B_w[i], i, wserial[0]

        def mm_group(mms, reads, bank, accumulate_into=False):
            w = deps(reads=reads, writes=[] if accumulate_into else [B_ps[bank]])
            if accumulate_into:
                w = w + deps(reads=[B_ps[bank]])
            tok = None
            for i, f in enumerate(mms):
                tok = P.emit("pe", f, waits=w if i == 0 else (), sig=(i == len(mms) - 1))
            commit(tok, reads=reads)
            B_ps[bank].w = [tok]
            if not accumulate_into:
                B_ps[bank].r = []
            return tok

        def MM(out, lhsT, rhs, start, stop, tp=None):
            if tp is None:
                return lambda e: e.matmul(out, lhsT=lhsT, rhs=rhs, start=start, stop=stop)
            return lambda e: e.matmul(out, lhsT=lhsT, rhs=rhs, start=start, stop=stop, tile_position=tp,
                                      skip_group_check=True)

        def op(eng, fn, reads=(), writes=()):
            tok = P.emit(eng, fn, waits=deps(reads=reads, writes=writes))
            commit(tok, reads=reads, writes=writes)
            return tok

        ptoks = []
        t_c1 = P.dma("pool", cvec, cvec_d, "tb")
        t_tb = P.dma("pool", gfin, gfin_d, "tb")
        t = P.dma("pool", VF(o_ebA, 8 * 3 * 128), tabA_d, "tb2")
        t = P.dma("pool", VF(o_ebB, 2 * 4 * 3 * 128), tabB_d, "tb2")
        t_tb2 = P.dma("pool", VF(o_eb2, 5 * 4 * 128), tab2_d, "tb2")
        t0 = P.emit("pool", lambda e: e.memset(identf, 0.0))
        t1 = P.emit("pool", lambda e: e.affine_select(out=identf, in_=identf, pattern=[[-1, 128]], compare_op=ALU.not_equal,
                                                      fill=1.0, base=0, channel_multiplier=1), waits=[t0])
        ptoks.append(P.emit("act", lambda e: e.activation(out=ident, in_=identf, func=AF.Copy), waits=[t1]))
        ptoks.append(P.emit("dve", lambda e: e.memset(ones, 1.0)))
        ptoks.append(P.emit("dve", lambda e: e.memset(zeros, 0.0)))
        ptoks.append(P.emit("act", lambda e: e.activation(out=esink, in_=sinkp, func=AF.Exp), waits=[t_tb]))
        ptoks.append(t_tb)
        for (o_, n_) in ((o_ebA, 3072), (o_ebB, 3072), (o_eb2, 2560)):
            v = VF(o_, n_)
            B_eb.w.append(P.emit("act", (lambda v: lambda e: e.activation(out=v, in_=v, func=AF.Exp))(v), waits=[t_tb2]))

        cast_tok = {}
        cast_next = [0]

        def cast_upto(slab):
            while cast_next[0] <= min(slab, NSLAB - 1):
                s_ = cast_next[0]
                cast_tok[s_] = P.dma("pool", wscr[s_ * 128:(s_ + 1) * 128, :], wsl[s_ * 128:(s_ + 1) * 128, :], "cs%d" % s_)
                cast_next[0] += 1

        cast_upto(3)

        for e_ in ENGS:
            P.emit(e_, None, waits=ptoks, sig=False)

        stat_ctr = [0]

        def stat_cols(n):
            c0 = stat_ctr[0]
            if c0 + 3 * n > 256:
                c0 = 0
            stat_ctr[0] = c0 + 3 * n
            return stat[:, c0:c0 + n], stat[:, c0 + n:c0 + 2 * n], stat[:, c0 + 2 * n:c0 + 3 * n]

        def norm_stats(xaps, xbufs, junks, jbufs):
            n = len(xaps)
            ms, sd, rs = stat_cols(n)
            toks = []
            for i in range(n):
                f = (lambda x_, m_, j_: lambda e: e.activation(out=j_, in_=x_, func=AF.Square, scale=1.0 / 32.0, accum_out=m_))(
                    xaps[i], ms[:, i:i + 1], junks[i])
                tk = P.emit("act", f, waits=deps(reads=[xbufs[i]], writes=[jbufs[i]]))
                commit(tk, reads=[xbufs[i]], writes=[jbufs[i]])
                toks.append(tk)
            t2 = P.emit("act", lambda e: e.activation(out=sd, in_=ms, func=AF.Sqrt, bias=EPS, scale=1.0), waits=toks)
            t3 = P.emit("dve", lambda e: e.reciprocal(out=rs, in_=sd), waits=[t2])
            return rs, t3

        def transpose_tile(hbi, t, hi, gcol0):
            b = acquire()
            mms = [(lambda k: lambda e: e.transpose(out=psum_b[b][:, k * 128:(k + 1) * 128], in_=hb[hbi][:, k * 128:(k + 1) * 128],
                                                    identity=ident))(k) for k in range(8)]
            mm_group(mms, [B_hb[hbi]], b)
            src = psum_b[b][:, 0:1024].rearrange("p (k t) -> p k t", k=8)
            dst = hTs[hi][:, :, t * 128:(t + 1) * 128]
            gb = bass.AP(arena, o_cvec // 4 + gcol0, [[AW, 128], [1, 8], [0, 128]])
            op("dve", lambda e: e.tensor_tensor(out=dst, in0=src, in1=gb, op=ALU.mult), reads=[B_ps[b], B_const], writes=[B_hTs[hi][t]])
            release(b)

        def fm_block(wv, wb, blk, evac, hi):
            b = acquire()
            wv3 = wv.rearrange("p (k c) -> p k c", k=8)
            mms = [MM(psum[b][:, :], wv3[:, k, blk * 128:(blk + 1) * 128], hTs[hi][:, k, :], k == 0, k == 7) for k in range(8)]
            mm_group(mms, [wb] + B_hTs[hi], b)
            evac(b)
            release(b)

        def xnorm_tasks(xd, c, hi, label):
            tasks = []
            st = {}
            for t in range(4):
                def ta_dma(t=t):
                    if t in st:
                        return
                    P.label = label
                    i = nxt("xt", 2)
                    st[t] = i
                    tk = P.dma("pool", xtmp[i], xd[c * T + t * 128:c * T + (t + 1) * 128, :], "xt%d" % i,
                               waits=deps(writes=[B_xtmp[i]]))
                    commit(tk, writes=[B_xtmp[i]])

                def ta(t=t, ta_dma=ta_dma):
                    ta_dma()
                    P.label = label
                    i = st[t]
                    rs, t3 = norm_stats([xtmp[i]], [B_xtmp[i]], [hb[i]], [B_hb[i]])
                    f = (lambda i_, r_: lambda e: e.tensor_scalar(out=hb[i_], in0=xtmp[i_], scalar1=r_, scalar2=None, op0=ALU.mult))(i, rs[:, 0:1])
                    tk = P.emit("dve", f, waits=[t3] + deps(reads=[B_xtmp[i]], writes=[B_hb[i]]))
                    commit(tk, reads=[B_xtmp[i]], writes=[B_hb[i]])
                ta.dma = ta_dma

                def tb(t=t):
                    P.label = label
                    transpose_tile(st[t], t, hi, 0)
                tasks += [ta, tb]
            return tasks

        cur_base = [0]

        def kv_tasks(xd, c, n, hi, base):
            s4, s5 = (c + base) % 4, (c + base) % 5
            hT = hTs[hi]
            B_hT = B_hTs[hi]
            tasks = xnorm_tasks(xd, c, hi, 'kv.norm')
            kblocks = [(0, 0), (0, 1), (0, 2), (0, 3), (1, 0), (1, 1), (1, 2)]
            cur = {}

            def slab(i):
                if i not in cur or wcontent.get(cur[i][2]) != cur[i][3]:
                    cur[i] = wget(S_KV0 + i)
                return cur[i][0], cur[i][1]

            for bi, (sl_, blk) in enumerate(kblocks):
                def tk_(bi=bi, sl_=sl_, blk=blk):
                    P.label = 'kv.K'
                    wv, wb = slab(sl_)
                    if bi < 5:
                        dst, dbuf = KTr[:, s4, bi, :], B_KT[s4]
                    else:
                        dst, dbuf = KT2r[:, s5, bi - 5, :], B_KT2[s5]

                    def evac(b):
                        tk = P.emit("act", lambda e: e.activation(out=dst, in_=psum[b][:, :], func=AF.Copy),
                                    waits=deps(reads=[B_ps[b]], writes=[dbuf]))
                        commit(tk, reads=[B_ps[b]])
                        dbuf.w.append(tk)

                    if bi == 0:
                        B_KT[s4].w, B_KT[s4].r = list(deps(writes=[B_KT[s4]])), []
                    if bi == 5:
                        B_KT2[s5].w, B_KT2[s5].r = list(deps(writes=[B_KT2[s5]])), []
                    fm_block(wv, wb, blk, evac, hi)
                tasks.append(tk_)

            def vtask(kind, idx):
                def tv():
                    P.label = 'kv.V'
                    if kind == 0:
                        wv, wb = slab(2)
                        wv3 = wv.rearrange("p (k c) -> p k c", k=8)
                        if idx == 0:
                            B_V[s4].w, B_V[s4].r = list(deps(writes=[B_V[s4]])), []
                        lw = [hT[:, k, idx * 128:(idx + 1) * 128] for k in range(8)]
                        rw = [wv3[:, k, 0:384] for k in range(8)]
                        ncol, dst, dbuf, rdh, eng = 384, Vr[:, s4, idx, 0:384], B_V[s4], [B_hT[idx]], "act"
                    elif kind == 1:
                        wv, wb = slab(3)
                        wv3 = wv.rearrange("p (k c) -> p k c", k=8)
                        lw = [hT[:, k, idx:512:4] for k in range(8)]
                        rw = [wv3[:, k, 0:256] for k in range(8)]
                        ncol, dst, dbuf, rdh, eng = 256, Vr[:, s4, idx, 384:640], B_V[s4], B_hT, "act"
                    else:
                        wv, wb = slab(3)
                        wv3 = wv.rearrange("p (k c) -> p k c", k=8)
                        if idx == 0:
                            B_V2[s5].w, B_V2[s5].r = list(deps(writes=[B_V2[s5]])), []
                        lw = [hT[:, k, idx:512:4] for k in range(8)]
                        rw = [wv3[:, k, 256:512] for k in range(8)]
                        ncol, dst, dbuf, rdh, eng = 256, V2r[:, s5, idx, :], B_V2[s5], B_hT, "act"
                    b = acquire()
                    mms = [MM(psum[b][:, 0:ncol], lw[k], rw[k], k == 0, k == 7) for k in range(8)]
                    mm_group(mms, [wb] + rdh, b)
                    if eng == "dve":
                        f = lambda e: e.tensor_copy(out=dst, in_=psum[b][:, 0:ncol])
                    else:
                        f = lambda e: e.activation(out=dst, in_=psum[b][:, 0:ncol], func=AF.Copy)
                    tk = P.emit(eng, f, waits=deps(reads=[B_ps[b]], writes=[dbuf]))
                    commit(tk, reads=[B_ps[b]])
                    dbuf.w.append(tk)
                    release(b)
                return tv

            vt = [[vtask(kind, idx) for idx in range(4)] for kind in range(3)]
            xn, kb = tasks[:8], tasks[8:]
            urgent = xn + kb[5:7] + vt[2]
            relaxed = kb[0:5] + vt[0] + vt[1]
            return urgent, relaxed

        def softmax_tile(bS, c0, c1, eb_ap, nbc=0):
            pi = nxt("PT", 8)
            src = psum[bS][:, c0:c1]
            op("act", lambda e: e.activation(out=src, in_=src, func=AF.Exp, scale=0.125), reads=[B_ps[bS]], writes=[B_ps[bS]])
            dstP = PTb[pi][:, c0:c1]
            if nbc:
                i0 = src.rearrange("p (r q) -> p r q", r=nbc)
                o0 = dstP.rearrange("p (r q) -> p r q", r=nbc)
                op("dve", lambda e: e.tensor_tensor(out=o0, in0=i0, in1=eb_ap, op=ALU.mult), reads=[B_ps[bS], B_eb], writes=[B_PT[pi]])
            else:
                op("dve", lambda e: e.tensor_tensor(out=dstP, in0=src, in1=eb_ap, op=ALU.mult), reads=[B_ps[bS], B_eb], writes=[B_PT[pi]])
            return pi

        class Job:
            __slots__ = ("s1", "s2", "s3", "pre3", "post3", "st")

            def __init__(self):
                self.pre3 = None
                self.post3 = None
                self.st = None

        def pv_emit(mo, rv, acc):
            bO, bD = acc["bO"], acc["bD"]
            w = deps(reads=rv + [B_const, B_ps[bO], B_ps[bD]])
            tok = None
            for i, f in enumerate(mo):
                tok = P.emit("pe", f, waits=w if i == 0 else (), sig=(i == len(mo) - 1))
            commit(tok, reads=rv)
            B_ps[bO].w = [tok]
            B_ps[bD].w = [tok]

        def banded_jobs(c, n, acc, hh, first, final, kt_blk, qt_blk, half, vcol, eb, dil):
            hs_ = slice(half * 64, half * 64 + 64)
            os_ = slice(hh * 64, hh * 64 + 64)
            cb = cur_base[0]
            jobs = []
            for t in range(4):
                if dil == 1:
                    Tg = 4 * c + t
                    keys = [(j, (Tg - 1 + j) // 4, (Tg - 1 + j) % 4) for j in range(3) if 0 <= Tg - 1 + j < 4 * n]
                    qsl = slice(t * 128, (t + 1) * 128)
                else:
                    keys = [(j, c - 1 + j, t) for j in range(3) if 0 <= c - 1 + j < n]
                    qsl = slice(t, 512, 4)
                jb = Job()

                def s1(keys=keys, qsl=qsl):
                    bS = acquire()
                    mms = []
                    rd_ = [B_QT[qt_blk]]
                    for (j, ck, tk) in keys:
                        ksl = slice(tk * 128, (tk + 1) * 128) if dil == 1 else slice(tk, 512, 4)
                        mms.append(MM(psum[bS][:, j * 128:(j + 1) * 128], KTr[hs_, (ck + cb) % 4, kt_blk, ksl], QT[hs_, qt_blk, qsl], True, True,
                                      tp=(half * 64, 0)))
                        rd_.append(B_KT[(ck + cb) % 4])
                    mm_group(mms, rd_, bS)
                    return bS

                def s2(bS, keys=keys):
                    j0, j1 = keys[0][0], keys[-1][0] + 1
                    pi = softmax_tile(bS, j0 * 128, j1 * 128, eb[:, j0 * 128:j1 * 128])
                    release(bS)
                    return pi

                def s3(pi, keys=keys, qsl=qsl):
                    bO, bD = acc["bO"], acc["bD"]
                    mo = []
                    rv = [B_PT[pi]]
                    for idx, (j, ck, tk) in enumerate(keys):
                        st = first and idx == 0
                        sp_ = final and idx == len(keys) - 1
                        rhs = PTb[pi][:, j * 128:(j + 1) * 128]
                        mo.append(MM(psum[bO][os_, qsl], Vr[:, (ck + cb) % 4, tk, vcol:vcol + 64], rhs, st, sp_, tp=(0, hh * 64)))
                        mo.append(MM(psum[bD][os_, qsl], ones[:, 0:64], rhs, st, sp_, tp=(0, hh * 64)))
                        rv.append(B_V[(ck + cb) % 4])
                    pv_emit(mo, rv, acc)

                jb.s1, jb.s2, jb.s3 = s1, s2, s3
                jobs.append(jb)
            return jobs

        def g2_jobs(c, n, acc, hh, pair, half, hs):
            hs_ = slice(half * 64, half * 64 + 64)
            os_ = slice(hh * 64, hh * 64 + 64)
            dls = [dl for dl in (-2, -1, 0, 1, 2) if 0 <= c + dl < n]
            cb = cur_base[0]
            jobs = []

            def rv4(ap2d, g4):
                return ap2d[:, g4:512:4]

            for dl in dls:
                ck = c + dl
                jb = Job()

                def s1(ck=ck):
                    bS = acquire()
                    mms = [MM(psum[bS][:, g4 * 128:(g4 + 1) * 128], rv4(KT2r[hs_, (ck + cb) % 5, pair, :], g4), rv4(QT[hs_, 8 + pair, :], g4),
                              True, True, tp=(half * 64, 0)) for g4 in range(4)]
                    mm_group(mms, [B_QT[8 + pair], B_KT2[(ck + cb) % 5]], bS)
                    return bS

                def s2(bS, dl=dl):
                    o2 = o_eb2 // 4 + ((dl + 2) * 4 + hs) * 128
                    eb = bass.AP(arena, o2, [[AW, 128], [0, 4], [1, 128]])
                    pi = softmax_tile(bS, 0, 512, eb, nbc=4)
                    release(bS)
                    return pi

                def s3(pi, ck=ck, dl=dl):
                    bO, bD = acc["bO"], acc["bD"]
                    last = dl == dls[-1]
                    mo = []
                    for g4 in range(4):
                        rhs = PTb[pi][:, g4 * 128:(g4 + 1) * 128]
                        mo.append(MM(rv4(psum[bO][os_, :], g4), V2r[:, (ck + cb) % 5, g4, hs * 64:hs * 64 + 64], rhs, False, last, tp=(0, hh * 64)))
                        mo.append(MM(rv4(psum[bD][os_, :], g4), ones[:, 0:64], rhs, False, last, tp=(0, hh * 64)))
                    pv_emit(mo, [B_PT[pi], B_V2[(ck + cb) % 5]], acc)

                jb.s1, jb.s2, jb.s3 = s1, s2, s3
                jobs.append(jb)
            return jobs

        def run_jobs(jobs, depth, fillers=None, G=2):
            fillers = fillers or {}
            groups = [jobs[i:i + G] for i in range(0, len(jobs), G)]
            ng = len(groups)
            fg = {}
            for st_, fs in fillers.items():
                fg.setdefault(st_ // G, []).extend(fs)
            for gi in range(ng + depth):
                if gi < ng:
                    for jb in groups[gi]:
                        jb.st = jb.s2(jb.s1())
                k = gi - depth
                if k >= 0:
                    for jb in groups[k]:
                        if jb.pre3 is not None:
                            jb.pre3()
                        jb.s3(jb.st)
                        if jb.post3 is not None:
                            jb.post3()
                for f_ in fg.get(gi, ()):
                    f_()
                    P.label = 'm.att'
            for gi in sorted(fg):
                if gi >= ng + depth:
                    for f_ in fg[gi]:
                        f_()

        def normalize(bO, bD, dst, dbuf, sink_col):
            ri = nxt("rd", 2)
            rd = rdb[ri]
            if sink_col is not None:
                op("act", lambda e: e.activation(out=rd, in_=psum[bD][:, :], func=AF.Identity, bias=sink_col, scale=1.0),
                   reads=[B_ps[bD], B_const], writes=[B_rd[ri]])
                op("dve", lambda e: e.reciprocal(out=rd, in_=rd), reads=[B_rd[ri]], writes=[B_rd[ri]])
            else:
                op("dve", lambda e: e.reciprocal(out=rd, in_=psum[bD][:, :]), reads=[B_ps[bD]], writes=[B_rd[ri]])
            op("dve", lambda e: e.tensor_tensor(out=dst, in0=psum[bO][:, :], in1=rd, op=ALU.mult), reads=[B_ps[bO], B_rd[ri]], writes=[dbuf])

        def main_stage(xd, yd, c, n, hi, prefetched, kvf, nextx, base):
            hT = hTs[hi]
            B_hT = B_hTs[hi]
            cur_base[0] = base
            P.label = 'm.norm1'
            if not prefetched:
                for f_ in xnorm_tasks(xd, c, hi, 'm.norm1'):
                    f_()
            if stop == 'load':
                return
            kvf = (list(kvf[0]), list(kvf[1])) if kvf else None
            if kvf:
                kvf[0][0].dma()
                kvf[0][2].dma()
            P.label = 'm.Q'
            qb = 0
            for sl_, nb in ((0, 4), (1, 4), (2, 2)):
                wv, wb, _, _ = wget(S_Q0 + sl_)
                for blk in range(nb):
                    def evac(b, qb=qb):
                        op("act", lambda e: e.activation(out=QT[:, qb, :], in_=psum[b][:, :], func=AF.Copy), reads=[B_ps[b]], writes=[B_QT[qb]])
                    fm_block(wv, wb, blk, evac, hi)
                    qb += 1
            if stop == 'q':
                return
            P.label = 'm.att'
            jobs = []
            for pair in range(4):
                acc = {}
                pj = []
                for hh in range(2):
                    h = 2 * pair + hh
                    pj += banded_jobs(c, n, acc, hh, True, True, 0, h % 4, h // 4, (h // 4) * 64, ebA[:, h, :], 1)

                def preA(acc=acc):
                    bO, bD = acquire(), acquire()
                    acc["bO"], acc["bD"] = bO, bD
                    B_ps[bO].w, B_ps[bO].r = list(deps(writes=[B_ps[bO]])), []
                    B_ps[bD].w, B_ps[bD].r = list(deps(writes=[B_ps[bD]])), []

                def postA(acc=acc, pair=pair):
                    normalize(acc["bO"], acc["bD"], oTA[:, pair, :], B_oTA[pair], esink[:, pair:pair + 1])
                    release(acc["bO"])
                    release(acc["bD"])

                pj[0].pre3 = preA
                pj[-1].post3 = postA
                jobs += pj
            for pair in range(2):
                acc = {}
                pj = []
                for hh in range(2):
                    hs = 2 * pair + hh
                    pj += banded_jobs(c, n, acc, hh, False, False, 1 + pair, 4 + pair, hh, 128 + hs * 64, ebB[:, 0, hs, :], 1)
                    pj += banded_jobs(c, n, acc, hh, False, False, 3 + pair, 6 + pair, hh, 384 + hs * 64, ebB[:, 1, hs, :], 4)
                    pj += g2_jobs(c, n, acc, hh, pair, hh, hs)

                def preB(acc=acc):
                    bO, bD = acquire(), acquire()
                    acc["bO"], acc["bD"] = bO, bD
                    mm_group([MM(psum[bO][:, :], zeros[:, 0:128], zeros[:, :], True, True)], [B_const], bO)
                    mm_group([MM(psum[bD][:, :], zeros[:, 0:128], zeros[:, :], True, True)], [B_const], bD)

                def postB(acc=acc, pair=pair):
                    normalize(acc["bO"], acc["bD"], oTB[:, pair, :], B_oTB[pair], None)
                    release(acc["bO"])
                    release(acc["bD"])

                pj[0].pre3 = preB
                pj[-1].post3 = postB
                jobs += pj
            urgent, relaxed = kvf if kvf else ([], [])
            late = relaxed[len(relaxed) - 8:] if relaxed else []
            early = relaxed[:len(relaxed) - len(late)]
            fl = {}
            if len(urgent) == 14:
                order = [0, 2, 1, 4, 3, 6, 5, 7, 8, 9, 10, 11, 12, 13]
                steps = [1, 2, 6, 7, 10, 11, 14, 17, 19, 21, 23, 25, 27, 29]
                for o_, st_ in zip(order, steps):
                    fl.setdefault(st_, []).append(urgent[o_])
            else:
                for q_, f_ in enumerate(urgent):
                    fl.setdefault(1 + 2 * q_, []).append(f_)
            for q_, f_ in enumerate(early):
                fl.setdefault(36 + 5 * q_, []).append(f_)
            run_jobs(jobs, 3, fl, G=2)
            if stop == 'attB':
                return
            if dbg_c == c:
                for nm, view, bufs in (("d_QT", VB(o_R, 5120), B_QT), ("d_oTA", VB(o_R + 10240, 2048), B_oTA), ("d_oTB", VB(o_R + 18432, 1024), B_oTB),
                                       ("d_hT", VB(o_hT, 4096), B_hT)):
                    tkd = P.dma("pool", dbgt[nm], view, nm, waits=deps(reads=bufs))
                    commit(tkd, reads=bufs + B_mT + B_aT)
                    for e_ in ENGS:
                        P.emit(e_, None, waits=[tkd], sig=False)
            P.label = 'm.merge'
            for t in range(4):
                tk = P.dma("pool", xres[t], xd[c * T + t * 128:c * T + (t + 1) * 128, :], "xr%d" % t, waits=deps(writes=[B_xres[t]]))
                commit(tk, writes=[B_xres[t]])
            def merge_block(j):
                wv, wb, _, _ = wget(S_M0 + j)
                wg = wv[:, 0:2048].rearrange("p (k c) -> p k c", k=8)
                wa = wv[:, 2048:2560].rearrange("p (h c) -> p h c", h=4)
                wbb = wv[:, 2560:2816].rearrange("p (h c) -> p h c", h=2)
                bG = [acquire(), acquire()]
                for gi in range(2):
                    mms = [MM(psum[bG[gi]][:, :], wg[:, k, gi * 128:(gi + 1) * 128], hT[:, k, :], k == 0, k == 7) for k in range(8)]
                    mm_group(mms, [wb] + B_hT, bG[gi])
                bYa, bYb = acquire(), acquire()
                mms = [MM(psum[bYa][:, :], wa[:, p_, :], oTA[:, p_, :], p_ == 0, p_ == 3) for p_ in range(4)]
                mm_group(mms, [wb] + B_oTA, bYa)
                mms = [MM(psum[bYb][:, :], wbb[:, p_, :], oTB[:, p_, :], p_ == 0, p_ == 1) for p_ in range(2)]
                mm_group(mms, [wb] + B_oTB, bYb)
                sis = []
                for gi in range(2):
                    si = nxt("sg", 4)
                    sis.append(si)
                    bcol = bg[:, gi * 8 + j:gi * 8 + j + 1]
                    op("act", (lambda si_, b_, bc: lambda e: e.activation(out=sgb[si_], in_=psum[b_][:, :], func=AF.Sigmoid, bias=bc, scale=1.0))(
                        si, bG[gi], bcol), reads=[B_ps[bG[gi]], B_const], writes=[B_sg[si]])
                    release(bG[gi])
                op("dve", lambda e: e.tensor_tensor(out=sgb[sis[0]], in0=psum[bYa][:, :], in1=sgb[sis[0]], op=ALU.mult),
                   reads=[B_ps[bYa], B_sg[sis[0]]], writes=[B_sg[sis[0]]])
                op("dve", lambda e: e.tensor_tensor(out=sgb[sis[1]], in0=psum[bYb][:, :], in1=sgb[sis[1]], op=ALU.mult),
                   reads=[B_ps[bYb], B_sg[sis[1]]], writes=[B_sg[sis[1]]])
                release(bYa)
                release(bYb)
                op("dve" if j == 7 else "pool", lambda e: e.tensor_tensor(out=mergedT[:, j, :], in0=sgb[sis[0]], in1=sgb[sis[1]], op=ALU.add),
                   reads=[B_sg[sis[0]], B_sg[sis[1]]], writes=[B_mT[j]])

            for j in range(8):
                merge_block(j)
            if stop == 'merge':
                return
            if dbg_c == c:
                tkd = P.dma("pool", dbgt["d_mT"], VB(o_R, 4096), "d_mT", waits=deps(reads=B_mT))
                commit(tkd, reads=B_mT + B_aT)
                for e_ in ENGS:
                    P.emit(e_, None, waits=[tkd], sig=False)
            P.label = 'm.out'
            for half in range(2):
                wv, wb, _, _ = wget(S_O0 + half)
                wv3 = wv.rearrange("p (k c) -> p k c", k=8)
                for t in range(4):
                    b = acquire()
                    mms = [MM(psum[b][:, :], mergedT[:, k, t * 128:(t + 1) * 128], wv3[:, k, :], k == 0, k == 7) for k in range(8)]
                    mm_group(mms, [wb] + B_mT, b)
                    xs_ = xres[t][:, half * 512:(half + 1) * 512]
                    op("dve", (lambda x_, b_: lambda e: e.tensor_tensor(out=x_, in0=psum[b_][:, :], in1=x_, op=ALU.add))(xs_, b),
                       reads=[B_ps[b], B_xres[t]], writes=[B_xres[t]])
                    release(b)
            if stop == 'out':
                return
            if dbg_c == c:
                tkd = P.dma("pool", dbgt["d_x1"], VF(o_xres, 4096), "d_x1", waits=deps(reads=B_xres))
                commit(tkd, reads=B_xres)
                for e_ in ENGS:
                    P.emit(e_, None, waits=[tkd], sig=False)
            P.label = 'm.norm2'
            his = []
            for t in range(4):
                i = nxt("xt", 2)
                his.append(i)
                rs, t3 = norm_stats([xres[t]], [B_xres[t]], [hb[i]], [B_hb[i]])
                f = (lambda i_, t_, r_: lambda e: e.tensor_scalar(out=hb[i_], in0=xres[t_], scalar1=r_, scalar2=None, op0=ALU.mult))(
                    i, t, rs[:, 0:1])
                tk = P.emit("dve", f, waits=[t3] + deps(reads=[B_xres[t]], writes=[B_hb[i]]))
                commit(tk, reads=[B_xres[t]], writes=[B_hb[i]])
                if t < 2:
                    for f_ in late[t * 4:(t + 1) * 4]:
                        f_()
                    P.label = 'm.norm2'
                if t >= 1:
                    transpose_tile(his[t - 1], t - 1, hi, 8)
            transpose_tile(his[3], 3, hi, 8)
            if stop == 'norm2':
                return
            P.label = 'm.up'
            pre = xnorm_tasks(nextx[0], nextx[1], 1 - hi, 'm.pre') if nextx is not None else []
            if pre:
                pre = [pre[i_] for i_ in (0, 2, 1, 4, 3, 6, 5, 7)]
            for ui in range(8):
                wv, wb, _, _ = wget(S_U0 + ui)
                for blk in range(4):
                    fb = ui * 4 + blk

                    def evac(b, fb=fb):
                        ri = nxt("rel", 2)
                        op("act", lambda e: e.activation(out=relb[ri], in_=psum[b][:, :], func=AF.Relu), reads=[B_ps[b]], writes=[B_rel[ri]])
                        op("dve", lambda e: e.tensor_tensor(out=aT[:, fb, :], in0=relb[ri], in1=relb[ri], op=ALU.mult),
                           reads=[B_rel[ri]], writes=[B_aT[fb]])
                    fm_block(wv, wb, blk, evac, hi)
                    if pre and fb % 3 == 2:
                        pre.pop(0)()
                        P.label = 'm.up'
            while pre:
                pre.pop(0)()
            if stop == 'up':
                return
            if dbg_c == c:
                tkd = P.dma("pool", dbgt["d_aT"], VB(o_R, 16384), "d_aT", waits=deps(reads=B_aT))
                commit(tkd, reads=B_aT)
                for e_ in ENGS:
                    P.emit(e_, None, waits=[tkd], sig=False)
            P.label = 'm.down'
            for half in range(2):
                bt = [acquire() for _ in range(4)]
                for t in range(4):
                    B_ps[bt[t]].w, B_ps[bt[t]].r = list(deps(writes=[B_ps[bt[t]]])), []
                for kg in range(4):
                    wv, wb, _, _ = wget(S_D0 + half * 4 + kg)
                    wv3 = wv.rearrange("p (k c) -> p k c", k=8)
                    mms = []
                    for kk in range(8):
                        fb = kg * 8 + kk
                        for t in range(4):
                            mms.append(MM(psum[bt[t]][:, :], aT[:, fb, t * 128:(t + 1) * 128], wv3[:, kk, :], fb == 0, fb == 31))
                    w = deps(reads=[wb] + B_aT[kg * 8:kg * 8 + 8] + [B_ps[bt[t]] for t in range(4)])
                    tok = None
                    for i, f in enumerate(mms):
                        tok = P.emit("pe", f, waits=w if i == 0 else (), sig=(i == len(mms) - 1))
                    commit(tok, reads=[wb] + B_aT[kg * 8:kg * 8 + 8])
                    for t in range(4):
                        B_ps[bt[t]].w = [tok]
                for t in range(4):
                    xs_ = xres[t][:, half * 512:(half + 1) * 512]
                    op("dve", (lambda x_, b_: lambda e: e.tensor_tensor(out=x_, in0=psum[b_][:, :], in1=x_, op=ALU.add))(xs_, bt[t]),
                       reads=[B_ps[bt[t]], B_xres[t]], writes=[B_xres[t]])
                    release(bt[t])
            if stop == 'down':
                return
            P.label = 'm.final'
            rs, t3 = norm_stats(xres, B_xres, [hb[t % 2] for t in range(4)], [B_hb[t % 2] for t in range(4)])
            for t in range(4):
                f = (lambda t_, r_: lambda e: e.scalar_tensor_tensor(out=xres[t_], in0=xres[t_], scalar=r_, in1=gfin, op0=ALU.mult, op1=ALU.mult))(
                    t, rs[:, t:t + 1])
                tk = P.emit("dve", f, waits=[t3] + deps(reads=[B_const], writes=[B_xres[t]]))
                commit(tk, writes=[B_xres[t]])
                ty = P.dma("pool", yd[c * T + t * 128:c * T + (t + 1) * 128, :], xres[t], "ys%d" % t, waits=[tk])
                commit(ty, reads=[B_xres[t]])

        sq = []
        base_ = 0
        for name, yname, S in seqs:
            assert (S // T) % 2 == 0
            sq.append((dr[name], dr[yname], S // T, base_))
            base_ += S // T
        for si, (xd, yd, n, base) in enumerate(sq):
            if stop == "pro":
                break
            nxt_seq = sq[si + 1] if si + 1 < len(sq) else None
            if si == 0:
                for cc in range(min(2, n)):
                    u_, r_ = kv_tasks(xd, cc, n, 1, base)
                    for f_ in u_ + r_:
                        f_()
                    if stop == "kv":
                        break
                if stop == "kv":
                    break
            for c in range(n):
                hi = c % 2
                if c + 2 < n:
                    kvf = kv_tasks(xd, c + 2, n, 1 - hi, base)
                elif nxt_seq is not None:
                    kvf = kv_tasks(nxt_seq[0], c + 2 - n, nxt_seq[2], 1 - hi, nxt_seq[3])
                else:
                    kvf = None
                if c + 1 < n:
                    nextx = (xd, c + 1)
                elif nxt_seq is not None:
                    nextx = (nxt_seq[0], 0)
                else:
                    nextx = None
                main_stage(xd, yd, c, n, hi, not (si == 0 and c == 0), kvf, nextx, base)
                if stop is not None:
                    break
            if stop is not None:
                break
        fin = [("ys%d" % t, P.cnt["ys%d" % t]) for t in range(4)] + [(nm, P.cnt[nm]) for nm in ("d_QT", "d_oTA", "d_oTB", "d_hT", "d_mT", "d_x1", "d_aT")]
        P.emit("pool", None, waits=fin, sig=False)
        P.emit("sp", None, waits=fin, sig=False)

        import os as _os
        if _os.environ.get("KPROF"):
            import json as _json
            _json.dump({"semorder": P.semorder, "tokmap": [[k[0], k[1], v] for k, v in P.tokmap.items()]}, open(_os.environ["KPROF"], "w"))
        block = es.enter_context(nc.Block())

        @block.sync
        def _(e):
            for f in P.ops["sp"]:
                f(e)

        @block.gpsimd
        def _(e):
            for f in P.ops["pool"]:
                f(e)

        @block.scalar
        def _(e):
            for f in P.ops["act"]:
                f(e)

        @block.vector
        def _(e):
            for f in P.ops["dve"]:
                f(e)

        @block.tensor
        def _(e):
            for f in P.ops["pe"]:
                f(e)
    return nc


_SEQS = [("xp", "yp", SEQ_P), ("xs", "ys", SEQ_S)]


def kernel(x_prompt, x_sample, rel_bias, g_mix, w_in, b_gate, w_branch_a, w_branch_b, w_out,
           attn_sink, g_mlp, w_up, w_down, g_final):
    f = lambda a: np.ascontiguousarray(np.asarray(a, dtype=np.float32))
    x_prompt, x_sample = f(x_prompt), f(x_sample)
    tabA, tabB, tab2 = _bias_tables(f(rel_bias))
    wsl = _slabs(f(w_in)[0], f(w_branch_a)[0], f(w_branch_b)[0], f(w_out)[0], f(w_up)[0], f(w_down)[0])
    cvec = np.zeros((128, 64), np.float32)
    cvec[:, 0:8] = f(g_mix)[0].reshape(8, 128).T
    cvec[:, 8:16] = f(g_mlp)[0].reshape(8, 128).T
    cvec[:, 16:32] = f(b_gate)[0].reshape(16, 128).T
    sk = f(attn_sink)[0]
    for pair in range(4):
        cvec[0:64, 32 + pair] = sk[2 * pair]
        cvec[64:128, 32 + pair] = sk[2 * pair + 1]
    gfin = np.ascontiguousarray(np.broadcast_to(f(g_final)[None, :], (128, D)))
    nc = build_program(_SEQS)
    in_maps = []
    for i in range(NCORES):
        in_maps.append({"xp": x_prompt[i], "xs": x_sample[i], "wsl": wsl, "tabA": tabA, "tabB": tabB, "tab2": tab2,
                        "cvec": cvec, "gfin": gfin})
    res = run_bass_kernel_spmd(nc, in_maps, core_ids=list(range(NCORES)))
    yp = np.stack([np.asarray(r["yp"], dtype=np.float32) for r in res.results], axis=0)
    ys = np.stack([np.asarray(r["ys"], dtype=np.float32) for r in res.results], axis=0)
    return (yp, ys)
```
